# Optimizing a Trainium2 kernel written in Bass

```python
import jax, jax.numpy as jnp
from jax import lax
import numpy as np

D_MODEL = 1024
BATCH = 8
SEQ = 2048
DEPTH = 4

HEAD_DIM = 64
DIL_PAIRS = ((128, 1), (512, 4), (2048, 16))
N_DIL_GROUPS = len(DIL_PAIRS)
DIL_HEADS_PER_GROUP = 4
DIL_HEADS = N_DIL_GROUPS * DIL_HEADS_PER_GROUP
SB_HEADS = 4
DIL_WIDTH = DIL_HEADS * HEAD_DIM
SB_WIDTH = SB_HEADS * HEAD_DIM
DIL_OUT_WIDTH = DIL_HEADS_PER_GROUP * HEAD_DIM
N_BRANCHES = 2
IN_COLS = 3 * DIL_WIDTH + 3 * SB_WIDTH + N_BRANCHES * D_MODEL
D_FF = 2816
CONV_WIDTH = 3
ROPE_THETA = 500000.0
ROPE_DIM = HEAD_DIM // 4
Q_BLOCK = 128
NORM_EPS = 1e-5
MAX_POS_OFFSET = 4096
MASK_VALUE = -1e30

kernel_name = "hybrid_dilated_stickbreaking_convffn"


def rms_norm(x, gain):
    xf = x.astype(jnp.float32)
    xf = xf * lax.rsqrt(jnp.mean(xf * xf, axis=-1, keepdims=True) + NORM_EPS)
    return (xf * gain.astype(jnp.float32)).astype(x.dtype)


def rope_tables(positions, dtype):
    inv_freq = ROPE_THETA ** (-jnp.arange(0, ROPE_DIM, 2, dtype=jnp.float32) / ROPE_DIM)
    ang = positions.astype(jnp.float32)[..., None] * inv_freq
    return (jnp.cos(ang)[:, :, None, None, :].astype(dtype),
            jnp.sin(ang)[:, :, None, None, :].astype(dtype))


def apply_partial_rope(x, cos, sin):
    half = ROPE_DIM // 2
    x1, x2, rest = x[..., :half], x[..., half:ROPE_DIM], x[..., ROPE_DIM:]
    return jnp.concatenate([x1 * cos - x2 * sin, x2 * cos + x1 * sin, rest], axis=-1)


def dilated_attention(q, k, v):
    b, s = q.shape[0], q.shape[1]
    n_blocks = s // Q_BLOCK
    scale = HEAD_DIM ** -0.5
    qs = [q[:, :, g] for g in range(N_DIL_GROUPS)]
    ks = [k[:, :, g] for g in range(N_DIL_GROUPS)]
    vs = [v[:, :, g] for g in range(N_DIL_GROUPS)]

    def block(t0):
        t = t0 + jnp.arange(Q_BLOCK)
        outs, lses = [], []
        for g, (window, dil) in enumerate(DIL_PAIRS):
            n_keys = window // dil + 1
            idx = t[:, None] - dil * jnp.arange(n_keys)[None, :]
            valid = idx >= 0
            idx = jnp.maximum(idx, 0)
            qg = lax.dynamic_slice_in_dim(qs[g], t0, Q_BLOCK, axis=1)
            kg = ks[g][:, idx]
            vg = vs[g][:, idx]
            sc = jnp.einsum('bqhd,bqkhd->bhqk', qg, kg).astype(jnp.float32) * scale
            sc = jnp.where(valid[None, None], sc, MASK_VALUE)
            m = jnp.max(sc, axis=-1, keepdims=True)
            p = jnp.exp(sc - m)
            den = jnp.sum(p, axis=-1)
            o = jnp.einsum('bhqk,bqkhd->bqhd', p, vg.astype(jnp.float32))
            o = o / jnp.transpose(den, (0, 2, 1))[..., None]
            outs.append(o)
            lses.append(m[..., 0] + jnp.log(den))
        o_all = jnp.stack(outs, axis=0)
        w = jax.nn.softmax(jnp.stack(lses, axis=0), axis=0)
        w = jnp.transpose(w, (0, 1, 3, 2))[..., None]
        return jnp.sum(w * o_all, axis=0).astype(q.dtype)

    out = lax.map(block, jnp.arange(n_blocks) * Q_BLOCK)
    return jnp.moveaxis(out, 0, 1).reshape(b, s, DIL_OUT_WIDTH)


def stick_breaking_attention(q, k, v):
    b, s = q.shape[0], q.shape[1]
    n_blocks = s // Q_BLOCK
    scale = HEAD_DIM ** -0.5
    key_pos = jnp.arange(s)

    def block(t0):
        qb = lax.dynamic_slice_in_dim(q, t0, Q_BLOCK, axis=1)
        z = jnp.einsum('bqhd,bkhd->bhqk', qb, k).astype(jnp.float32) * scale
        t = t0 + jnp.arange(Q_BLOCK)
        causal = key_pos[None, :] < t[:, None]
        log_keep = jnp.where(causal, jax.nn.log_sigmoid(-z), 0.0)
        suffix = lax.cumsum(log_keep, axis=3, reverse=True) - log_keep
        attn = jnp.where(causal, jnp.exp(jax.nn.log_sigmoid(z) + suffix), 0.0)
        return jnp.einsum('bhqk,bkhd->bqhd', attn, v.astype(jnp.float32)).astype(q.dtype)

    out = lax.map(block, jnp.arange(n_blocks) * Q_BLOCK)
    return jnp.moveaxis(out, 0, 1).reshape(b, s, SB_WIDTH)


def conv_ffn(h, w_up, conv_w, conv_b, w_down):
    up = h @ w_up
    a, bval = up[..., :D_FF], up[..., D_FF:]
    a = lax.conv_general_dilated(a, conv_w[:, None, :].astype(a.dtype), window_strides=(1,),
                                 padding=((CONV_WIDTH - 1, 0),),
                                 dimension_numbers=('NWC', 'WIO', 'NWC'),
                                 feature_group_count=D_FF) + conv_b
    return (jax.nn.silu(a) * bval) @ w_down


def setup_inputs(seed: int = 0) -> dict:
    key = jax.random.key(seed)
    ks = jax.random.split(key, 16)
    f32 = jnp.float32
    nrm = lambda k, shape, s: jax.random.normal(k, shape, f32) * s
    x = jax.random.normal(ks[0], (BATCH, SEQ, D_MODEL), f32)
    offset = jax.random.randint(ks[1], (BATCH, 1), 0, MAX_POS_OFFSET, dtype=jnp.int32)
    positions = offset + jnp.arange(SEQ, dtype=jnp.int32)[None, :]
    return {
        "x": x,
        "positions": positions,
        "norm_mix": 1.0 + nrm(ks[2], (DEPTH, D_MODEL), 0.05),
        "w_in": nrm(ks[3], (DEPTH, D_MODEL, IN_COLS), D_MODEL ** -0.5),
        "b_gate": nrm(ks[4], (DEPTH, N_BRANCHES * D_MODEL), 0.1),
        "w_proj_a": nrm(ks[5], (DEPTH, DIL_OUT_WIDTH, D_MODEL), DIL_OUT_WIDTH ** -0.5),
        "w_proj_b": nrm(ks[6], (DEPTH, SB_WIDTH, D_MODEL), SB_WIDTH ** -0.5),
        "w_out": nrm(ks[7], (DEPTH, D_MODEL, D_MODEL), D_MODEL ** -0.5),
        "norm_ffn": 1.0 + nrm(ks[8], (DEPTH, D_MODEL), 0.05),
        "w_up": nrm(ks[9], (DEPTH, D_MODEL, 2 * D_FF), D_MODEL ** -0.5),
        "conv_w": nrm(ks[10], (DEPTH, CONV_WIDTH, D_FF), CONV_WIDTH ** -0.5),
        "conv_b": nrm(ks[11], (DEPTH, D_FF), 0.02),
        "w_down": nrm(ks[12], (DEPTH, D_FF, D_MODEL), D_FF ** -0.5),
        "norm_final": 1.0 + nrm(ks[13], (D_MODEL,), 0.05),
    }


def reference(x, positions, norm_mix, w_in, b_gate, w_proj_a, w_proj_b, w_out,
              norm_ffn, w_up, conv_w, conv_b, w_down, norm_final):
    b, s, _ = x.shape
    cos, sin = rope_tables(positions, x.dtype)
    splits = np.cumsum([DIL_WIDTH, DIL_WIDTH, DIL_WIDTH, SB_WIDTH, SB_WIDTH, SB_WIDTH]).tolist()
    for l in range(DEPTH):
        h = rms_norm(x, norm_mix[l])
        proj = h @ w_in[l]
        dq, dk, dv, sq, sk, sv, gate_logits = jnp.split(proj, splits, axis=-1)
        dshape = (b, s, N_DIL_GROUPS, DIL_HEADS_PER_GROUP, HEAD_DIM)
        dq = apply_partial_rope(dq.reshape(dshape), cos, sin)
        dk = apply_partial_rope(dk.reshape(dshape), cos, sin)
        dv = dv.reshape(dshape)
        sshape = (b, s, SB_HEADS, HEAD_DIM)
        y_a = dilated_attention(dq, dk, dv) @ w_proj_a[l]
        y_b = stick_breaking_attention(sq.reshape(sshape), sk.reshape(sshape), sv.reshape(sshape)) @ w_proj_b[l]
        gates = jax.nn.sigmoid(gate_logits + b_gate[l])
        mixed = gates[..., :D_MODEL] * y_a + gates[..., D_MODEL:] * y_b
        x = x + mixed @ w_out[l]
        h = rms_norm(x, norm_ffn[l])
        x = x + conv_ffn(h, w_up[l], conv_w[l], conv_b[l], w_down[l])
    return rms_norm(x, norm_final)
```

```python
import contextlib
import math

import numpy as np
import ml_dtypes

import concourse.bass as bass
import concourse.mybir as mybir
from concourse.bass_utils import run_bass_kernel_spmd

F32 = mybir.dt.float32
BF16 = mybir.dt.bfloat16
I32 = mybir.dt.int32
AF = mybir.ActivationFunctionType
ALU = mybir.AluOpType

D = 1024
S = 2048
DEPTH = 4
NC8 = 8
IN_COLS = 5120
D_FF = 2816
NFC = 22
EPS = 1e-5
NEG = -30000.0
DIL = ((128, 1), (512, 4), (2048, 16))
FF_PARTS = ((0, 6), (6, 6), (12, 5), (17, 5))
SLOT = 2048
NSLOT = 3


class Buf:
    __slots__ = ("name", "w", "r")

    def __init__(self, name=""):
        self.name = name
        self.w = None
        self.r = []


class Chan:
    def __init__(self, sem):
        self.sem = sem
        self.count = 0


class Op:
    __slots__ = ("eng", "fn", "deps", "ticket", "inc", "chan", "chan_val", "waits")


class Sched:
    ENGS = ("pe", "act", "dve", "pool", "sp")

    def __init__(self):
        self.ops = {e: [] for e in self.ENGS}
        self.all = []
        self.extra = []

    def barrier(self):
        self.extra = [self.ops[e][-1] for e in self.ENGS if self.ops[e]]

    def op(self, eng, meth, reads=(), writes=(), chan=None, nobar=False, **kw):
        o = Op()
        o.eng, o.fn, o.chan = eng, (meth, kw), chan
        o.chan_val = None
        o.inc = False
        o.ticket = None
        deps = [] if nobar else list(self.extra)
        for b in reads:
            if b.w is not None:
                deps.append(b.w)
        for b in writes:
            if b.w is not None:
                deps.append(b.w)
            last = {}
            for r in b.r:
                last[(r.eng, id(r.chan) if r.chan is not None else 0)] = r
            deps.extend(last.values())
        o.deps = deps
        for b in reads:
            b.r.append(o)
        for b in writes:
            b.w = o
            b.r = []
        if chan is not None:
            chan.count += 16
            o.chan_val = chan.count
        self.ops[eng].append(o)
        self.all.append(o)
        return o

    EPOCH = 1500

    def finalize(self, alloc):
        for o in self.all:
            for d in o.deps:
                if d is o or d.chan is not None:
                    continue
                if d.eng == "pe" and o.eng == "pe":
                    continue
                d.inc = True
        sems = {}
        for e in self.ENGS:
            t = 0
            for o in self.ops[e]:
                if o.chan is None and o.inc:
                    t += 1
                    o.ticket = t
            sems[e] = [alloc(f"s_{e}_{i}") for i in range((t + self.EPOCH - 1) // self.EPOCH)]
        E = self.EPOCH
        for e in self.ENGS:
            seen = {}
            for o in self.ops[e]:
                need = {}
                for d in o.deps:
                    if d is o:
                        continue
                    if d.chan is not None:
                        key, sem, val, gval = ("c", id(d.chan)), d.chan.sem, d.chan_val, d.chan_val
                    else:
                        if d.eng == "pe" and o.eng == "pe":
                            continue
                        key, gval = ("e", d.eng), d.ticket
                        sem, val = sems[d.eng][(gval - 1) // E], (gval - 1) % E + 1
                    if seen.get(key, 0) >= gval:
                        continue
                    if key not in need or need[key][2] < gval:
                        need[key] = (sem, val, gval)
                o.waits = [(sem, val) for (sem, val, _) in need.values()]
                for key, (sem, val, gval) in need.items():
                    seen[key] = gval
                o.deps = None
        self.sems = sems

    def run(self, e, eng):
        sems = self.sems
        E = self.EPOCH
        for o in self.ops[e]:
            for sem, val in o.waits:
                eng.wait_ge(sem, val)
            meth, kw = o.fn
            ins = getattr(eng, meth)(**kw)
            if o.chan is not None:
                ins.then_inc(o.chan.sem, 16)
            elif o.inc:
                ins.then_inc(sems[e][(o.ticket - 1) // E], 1)


class Rot:
    def __init__(self, items):
        self.items = items
        self.i = 0

    def next(self):
        it = self.items[self.i % len(self.items)]
        self.i += 1
        return it


def _consts():
    c = {}
    eye = np.eye(128, dtype=np.float32)
    perm = np.zeros((128, 128), np.float32)
    for hb in (0, 64):
        for i in range(8):
            perm[hb + 8 + i, hb + i] = 1.0
            perm[hb + i, hb + 8 + i] = 1.0
    k = np.arange(128)[:, None]
    m = np.arange(128)[None, :]
    tri = (k >= m).astype(np.float32)
    mats = np.stack([eye, -eye, perm, tri, np.ones((128, 128), np.float32)], axis=1)
    c["cmats"] = np.ascontiguousarray(mats.reshape(128, 5 * 128))
    p = np.arange(128)[:, None]
    msk = []
    for (W, dil), width in zip(DIL, (256, 640, 640)):
        x = np.arange(width)[None, :]
        dlt = x - p
        valid = (dlt >= 0) & (dlt <= W) & (dlt % dil == 0)
        msk.append(np.where(valid, 0.0, NEG).astype(np.float32))
    x = np.arange(896)[None, :]
    msk.append(np.where(x - 384 > p, 0.0, NEG).astype(np.float32))
    c["cmask"] = np.ascontiguousarray(np.concatenate(msk, axis=1))
    invf = (500000.0 ** (-np.arange(0, 16, 2, dtype=np.float32) / 16.0)).astype(np.float32)
    col = np.zeros((128, 2), np.float32)
    for hb in (0, 64):
        col[hb:hb + 8, 0] = invf
        col[hb + 8:hb + 16, 0] = invf
        col[hb:hb + 8, 1] = -1.0
        col[hb + 8:hb + 16, 1] = 1.0
    c["ccol"] = col
    return c


MOFF = (0, 256, 896, 1536)
MW = 2432


class StopBuild(Exception):
    pass


class Safe(contextlib.ExitStack):
    def __exit__(self, et, ev, tb):
        super().__exit__(None, None, None)
        return False


def build(n_layers=DEPTH, stop=99):
    nc = bass.Bass("TRN2", target_bir_lowering=False)

    def din(name, shape, dt=F32):
        return nc.dram_tensor(name, list(shape), dt, kind="ExternalInput").ap()

    xT_d = din("xT", [D, S])
    pos_d = din("pos", [1, S], I32)
    gains_d = din("gains_h", [128, (2 * DEPTH + 1) * 8])
    bgate_d = din("bgate_h", [128, DEPTH * 16])
    convw_d = din("convw_h", [128, DEPTH * 3 * NFC])
    convb_d = din("convb_h", [128, DEPTH * NFC])
    WSPEC = (("w_in", D, IN_COLS), ("w_proj_a", 256, D), ("w_proj_b", 256, D), ("w_out", D, D), ("w_up", D, 2 * D_FF), ("w_down", D_FF, D))
    gath_d = {(nm, l): din(f"{nm}_{l}", [rows, cols]) for nm, rows, cols in WSPEC for l in range(n_layers)}

    class _G:
        def __init__(self, nm):
            self.nm = nm

        def __getitem__(self, l):
            return gath_d[(self.nm, l)]
    win_d, wpa_d, wpb_d, wout_d, wup_d, wdn_d = (_G(nm) for nm, _, _ in WSPEC)
    Bgath = {k: Buf() for k in gath_d}
    cmats_d = din("cmats", [128, 640])
    cmask_d = din("cmask", [128, MW])
    ccol_d = din("ccol", [128, 2])
    outT_d = nc.dram_tensor("outT", [D, S], F32, kind="ExternalOutput").ap()
    dbg_d = nc.dram_tensor("dbg", [128, 4, S], F32, kind="ExternalOutput").ap() if stop < 99 else None

    SC = Sched()
    O = SC.op
    es = contextlib.ExitStack()
    with es:
        def sb(name, shape, dt):
            return es.enter_context(nc.sbuf_tensor(name, list(shape), dt))

        xT = sb("xT_sb", [128, 8, S], F32)
        hT = sb("hT_sb", [128, 8, S], BF16)
        ropeC = sb("ropeC", [128, S], BF16)
        ropeS = sb("ropeS", [128, S], BF16)
        cmats = sb("cmats_sb", [128, 5, 128], BF16)
        cmask = sb("cmask_sb", [128, MW], BF16)
        ccol = sb("ccol_sb", [128, 2], F32)
        gains = sb("gains", [128, 2 * DEPTH + 1, 8], F32)
        bgate = sb("bgate", [128, DEPTH, 16], F32)
        convw = sb("convw", [128, DEPTH, 3, NFC], F32)
        convb = sb("convb", [128, DEPTH, NFC], F32)
        ring = sb("ring", [128, NSLOT, SLOT], BF16)
        ident, negident, perm, tri, ones = (cmats[:, i, :] for i in range(5))

        psum = [es.enter_context(nc.psum_tensor(f"ps{i}", [128, 512], F32)) for i in range(8)]
        PA = Rot([(psum[i], Buf(f"ps{i}")) for i in range(5)])
        PB = Rot([(psum[i], Buf(f"ps{i}")) for i in range(5, 8)])

        slot_ch = [Chan(es.enter_context(nc.semaphore(f"slot{i}"))) for i in range(NSLOT)]
        slot_buf = [Buf(f"slot{i}") for i in range(NSLOT)]
        x_ch = [Chan(es.enter_context(nc.semaphore(f"xch{i}"))) for i in range(8)]
        init_ch = Chan(es.enter_context(nc.semaphore("initch")))
        out_ch = Chan(es.enter_context(nc.semaphore("outch")))
        bnc_ch = [Chan(es.enter_context(nc.semaphore(f"bnc{i}"))) for i in range(DEPTH)]

        Bx = [[Buf() for _ in range(4)] for _ in range(8)]
        Bh = [[Buf() for _ in range(4)] for _ in range(8)]
        Bconst = Buf("const")
        Brope = Buf("rope")

        def T4(t):
            return slice(t * 512, (t + 1) * 512)

        slot_i = [0]

        def load_unit(pieces):
            si = slot_i[0] % NSLOT
            slot_i[0] += 1
            b = slot_buf[si]
            lastr = {}
            for r in b.r:
                lastr[r.eng] = r
            old = ([b.w] if b.w is not None else []) + list(lastr.values())
            off = 0
            views = []
            last = None
            for (src, k, n, bsrc) in pieces:
                dst = ring[:, si, off:off + k * n].rearrange("p (k n) -> p k n", k=k)
                tmp = Buf()
                tmp.r = list(old)
                last = O("pool", "dma_start", reads=[bsrc], writes=[tmp], chan=slot_ch[si], nobar=True, out=dst, in_=src)
                views.append(dst)
                off += k * n
            assert off <= SLOT
            b.w = last
            b.r = []
            return views, b

        def wcols(wd, l, c0, n):
            return (wd[l].rearrange("(k p) n -> p k n", p=128)[:, :, c0:c0 + n], 8, n, Bgath[(wd.nm, l)])

        def vec8(src):
            return src.rearrange("(c p) -> p c", p=128)

        O("pool", "dma_start", writes=[Bconst], chan=init_ch, out=cmats[:].rearrange("p a b -> p (a b)"), in_=cmats_d)
        O("pool", "dma_start", writes=[Bconst], chan=init_ch, out=cmask[:], in_=cmask_d)
        O("sp", "dma_start", writes=[Bconst], chan=init_ch, out=ccol[:], in_=ccol_d)
        O("sp", "dma_start", writes=[Bconst], chan=init_ch, out=gains[:].rearrange("p a b -> p (a b)"), in_=gains_d)
        O("sp", "dma_start", writes=[Bconst], chan=init_ch, out=bgate[:].rearrange("p a b -> p (a b)"), in_=bgate_d)
        O("sp", "dma_start", writes=[Bconst], chan=init_ch, out=convw[:].rearrange("p a b c -> p (a b c)"), in_=convw_d)
        O("sp", "dma_start", writes=[Bconst], chan=init_ch, out=convb[:].rearrange("p a b -> p (a b)"), in_=convb_d)
        for c in range(8):
            O("sp", "dma_start", writes=Bx[c], chan=x_ch[c], out=xT[:, c, :], in_=xT_d[c * 128:(c + 1) * 128, :])

        with contextlib.ExitStack() as es0:
            posi = es0.enter_context(nc.sbuf_tensor("posi", [128, S], I32))
            ang = es0.enter_context(nc.sbuf_tensor("ang", [128, S], F32))
            ta = es0.enter_context(nc.sbuf_tensor("ta", [128, S], F32))
            tb = es0.enter_context(nc.sbuf_tensor("tb", [128, S], F32))
            ki = es0.enter_context(nc.sbuf_tensor("ki", [128, S], I32))
            Bp, Bang, Bta, Btb, Bki = Buf(), Buf(), Buf(), Buf(), Buf()
            O("sp", "dma_start", writes=[Bp], chan=init_ch, out=posi[:], in_=pos_d[0, :].partition_broadcast(128))
            O("dve", "tensor_copy", reads=[Bp], writes=[Bang], out=ang[:], in_=posi[:])
            O("dve", "tensor_scalar", reads=[Bang, Bconst], writes=[Bang], out=ang[:], in0=ang[:], scalar1=ccol[:, 0:1], scalar2=None, op0=ALU.mult)

            def sin_table(shift, dst, use_sign):
                O("dve", "tensor_scalar", reads=[Bang], writes=[Bta], out=ta[:], in0=ang[:], scalar1=shift, scalar2=1.0 / (2 * math.pi), op0=ALU.add, op1=ALU.mult)
                O("dve", "tensor_copy", reads=[Bta], writes=[Bki], out=ki[:], in_=ta[:])
                O("dve", "tensor_copy", reads=[Bki], writes=[Btb], out=tb[:], in_=ki[:])
                O("dve", "tensor_scalar", reads=[Bang], writes=[Bta], out=ta[:], in0=ang[:], scalar1=shift, scalar2=None, op0=ALU.add)
                O("dve", "scalar_tensor_tensor", reads=[Btb, Bta], writes=[Bta], out=ta[:], in0=tb[:], scalar=-2 * math.pi, in1=ta[:], op0=ALU.mult, op1=ALU.add)
                O("dve", "tensor_scalar", reads=[Bta], writes=[Btb], out=tb[:], in0=ta[:], scalar1=math.pi, scalar2=-2 * math.pi, op0=ALU.is_gt, op1=ALU.mult)
                O("dve", "tensor_tensor", reads=[Bta, Btb], writes=[Bta], out=ta[:], in0=ta[:], in1=tb[:], op=ALU.add)
                O("dve", "tensor_scalar", reads=[Bta], writes=[Bta], out=ta[:], in0=ta[:], scalar1=-3.1415925, scalar2=3.1415925, op0=ALU.max, op1=ALU.min)
                if use_sign:
                    O("act", "activation", reads=[Bta], writes=[Btb], out=tb[:], in_=ta[:], func=AF.Sin)
                    O("dve", "tensor_scalar", reads=[Btb, Bconst], writes=[Brope], out=dst[:], in0=tb[:], scalar1=ccol[:, 1:2], scalar2=None, op0=ALU.mult)
                else:
                    O("act", "activation", reads=[Bta], writes=[Brope], out=dst[:], in_=ta[:], func=AF.Sin)

            sin_table(0.0, ropeS, True)
            sin_table(math.pi / 2, ropeC, False)
        SC.barrier()

        def rmsnorm(wk):
            res = []
            for tt in range(4):
                bank, Bb = PA.next()
                for c in range(8):
                    sq, Bsq = wk["sq"].next()
                    O("act", "activation", reads=[Bx[c][tt]], writes=[Bsq], out=sq[:], in_=xT[:, c, T4(tt)], func=AF.Square)
                    O("pe", "matmul", reads=[Bsq, Bconst], writes=[Bb], out=bank[:], lhsT=ones, rhs=sq[:], start=(c == 0), stop=(c == 7))
                rt, Brt = wk["rt"].next()
                O("act", "activation", reads=[Bb], writes=[Brt], out=rt[:], in_=bank[:], func=AF.Sqrt, scale=1.0 / D, bias=EPS)
                O("dve", "reciprocal", reads=[Brt], writes=[Brt], out=rt[:], in_=rt[:])
                yield tt, rt, Brt

        def norm_to_h(gidx, wk):
            for tt, rt, Brt in rmsnorm(wk):
                for c in range(8):
                    O("dve", "scalar_tensor_tensor", reads=[Bx[c][tt], Brt, Bconst], writes=[Bh[c][tt]],
                      out=hT[:, c, T4(tt)], in0=xT[:, c, T4(tt)], scalar=gains[:, gidx, c:c + 1], in1=rt[:], op0=ALU.mult, op1=ALU.mult)

        def proj_fm(wview, tt, bank, Bb, Bw):
            for kc in range(8):
                O("pe", "matmul", reads=[Bw, Bh[kc][tt]], writes=[Bb], out=bank[:], lhsT=wview[:, kc, :], rhs=hT[:, kc, T4(tt)], start=(kc == 0), stop=(kc == 7))

        def proj_tm(wview, blk, bank, Bb, Bw, n):
            for kc in range(8):
                O("pe", "matmul", reads=[Bw, Bh[kc][blk // 4]], writes=[Bb], out=bank[:, 0:n], lhsT=hT[:, kc, blk * 128:(blk + 1) * 128], rhs=wview[:, kc, :], start=(kc == 0), stop=(kc == 7))

        def chk(stage):
            if stop <= stage:
                raise StopBuild()

        def layers():
          for l in range(n_layers):
            chk(0)
            for si_ in range(NSLOT):
                slot_ch[si_] = Chan(es.enter_context(nc.semaphore(f"slot{si_}_{l}")))
            if True:
              with Safe() as esm:
                  dil_out = esm.enter_context(nc.sbuf_tensor(f"dil_out_{l}", [128, 2, S], BF16))
                  sb_out = esm.enter_context(nc.sbuf_tensor(f"sb_out_{l}", [128, 2, S], BF16))
                  Bdo = [[Buf() for _ in range(4)] for _ in range(2)]
                  Bso = [[Buf() for _ in range(4)] for _ in range(2)]

                  with Safe() as es1:
                      def sb1(name, shape, dt):
                          return es1.enter_context(nc.sbuf_tensor(f"{name}_{l}", list(shape), dt))
                      qk = sb1("qk", [128, 6, S], BF16)
                      vb = sb1("vb", [128, 16, 6, 64], BF16)
                      Bqk = [[Buf() for _ in range(4)] for _ in range(6)]
                      Bv = [Buf() for _ in range(16)]
                      f32p = Rot([(sb1(f"f32p{i}", [128, 512], F32), Buf()) for i in range(3)])
                      wk = {"sq": Rot([(sb1(f"sq{i}", [128, 512], BF16), Buf()) for i in range(2)]), "rt": f32p}
                      qraw = Rot([(sb1(f"qraw{i}", [128, 512], BF16), Buf()) for i in range(2)])
                      pts = Rot([(sb1(f"pt{i}", [128, 512], BF16), Buf()) for i in range(3)])
                      ets = Rot([(sb1(f"et{i}", [128, 512], F32), Buf()) for i in range(2)])
                      wts = Rot([(sb1(f"wt{i}", [128, 512], BF16), Buf()) for i in range(3)])
                      wacc32 = [(sb1(f"wacc32_{j}", [128, 512], F32), Buf()) for j in range(2)]
                      waccb = [Rot([(sb1(f"waccb{j}_{i}", [128, 512], BF16), Buf()) for i in range(3)]) for j in range(2)]

                      norm_to_h(l, wk)
                      chk(1)

                      for pp in range(2):
                          for g in range(3):
                              (wq, wkk), Bw = load_unit([wcols(win_d, l, g * 256 + 128 * pp, 128), wcols(win_d, l, 768 + g * 256 + 128 * pp, 128)])
                              for ci, wv_ in ((g, wq), (3 + g, wkk)):
                                  for tt in range(4):
                                      bank, Bb = PA.next()
                                      proj_fm(wv_, tt, bank, Bb, Bw)
                                      qr, Bqr = qraw.next()
                                      O("act", "copy", reads=[Bb], writes=[Bqr], out=qr[:], in_=bank[:])
                                      bank2, Bb2 = PA.next()
                                      O("pe", "matmul", reads=[Bqr, Bconst], writes=[Bb2], out=bank2[:], lhsT=perm, rhs=qr[:], start=True, stop=True)
                                      t1, Bt1 = f32p.next()
                                      t2, Bt2 = f32p.next()
                                      O("dve", "tensor_tensor", reads=[Bqr, Brope], writes=[Bt1], out=t1[:], in0=qr[:], in1=ropeC[:, T4(tt)], op=ALU.mult)
                                      O("dve", "tensor_tensor", reads=[Bb2, Brope], writes=[Bt2], out=t2[:], in0=bank2[:], in1=ropeS[:, T4(tt)], op=ALU.mult)
                                      O("dve", "tensor_tensor", reads=[Bt1, Bt2], writes=[Bqk[ci][tt]], out=qk[:, ci, T4(tt)], in0=t1[:], in1=t2[:], op=ALU.add)
                          vv01, Bw01 = load_unit([wcols(win_d, l, 1536 + g * 256 + 128 * pp, 128) for g in range(2)])
                          vv2, Bw2 = load_unit([wcols(win_d, l, 1536 + 512 + 128 * pp, 128)])
                          vviews = [(vv01[0], Bw01), (vv01[1], Bw01), (vv2[0], Bw2)]
                          for blk in range(16):
                              for g in range(3):
                                  bank, Bb = PA.next()
                                  proj_tm(vviews[g][0], blk, bank, Bb, vviews[g][1], 128)
                                  O("act", "copy", reads=[Bb], writes=[Bv[blk]], out=vb[:, blk, 2 * g:2 * g + 2, :], in_=bank[:, 0:128].rearrange("p (h d) -> p h d", h=2))
                          chk(2)
                          for qt in range(4):
                              ob, Bob = PB.next()
                              db, Bdb = PB.next()
                              first = [True, True]
                              for g, (W, dil) in enumerate(DIL):
                                  kb_lo = max(0, (qt * 512 - W) // 128)
                                  kb_hi = qt * 4 + 3
                                  for kb in range(kb_lo, kb_hi + 1):
                                      qs = max(qt * 512, kb * 128)
                                      qe = min((qt + 1) * 512, kb * 128 + W + 128)
                                      n = qe - qs
                                      if n <= 0:
                                          continue
                                      x0 = qs - kb * 128
                                      if g == 2 and x0 >= 128:
                                          x0 = 128
                                      mo = MOFF[g] + x0
                                      c0 = qs - qt * 512
                                      for j in range(2):
                                          r0 = j * 64
                                          bank, Bb = PA.next()
                                          O("pe", "matmul", reads=[Bqk[3 + g][kb // 4], Bqk[g][qt]], writes=[Bb], out=bank[:, 0:n],
                                            lhsT=qk[r0:r0 + 64, 3 + g, kb * 128:(kb + 1) * 128], rhs=qk[r0:r0 + 64, g, qs:qe], start=True, stop=False)
                                          O("pe", "matmul", reads=[Bconst], writes=[Bb], out=bank[:, 0:n], lhsT=ident, rhs=cmask[:, mo:mo + n], start=False, stop=True)
                                          pt, Bpt = pts.next()
                                          O("act", "activation", reads=[Bb], writes=[Bpt], out=pt[:, 0:n], in_=bank[:, 0:n], func=AF.Exp, scale=0.125)
                                          st = first[j]
                                          first[j] = False
                                          O("pe", "matmul", reads=[Bpt, Bv[kb]], writes=[Bob], out=ob[r0:r0 + 64, c0:c0 + n], lhsT=vb[:, kb, 2 * g + j, :], rhs=pt[:, 0:n],
                                            start=st, stop=False, skip_group_check=True, tile_position=(0, r0))
                                          O("pe", "matmul", reads=[Bpt, Bconst], writes=[Bdb], out=db[r0:r0 + 64, c0:c0 + n], lhsT=ones[:, 0:64], rhs=pt[:, 0:n],
                                            start=st, stop=False, skip_group_check=True, tile_position=(0, r0))
                              rd, Brd = f32p.next()
                              O("dve", "reciprocal", reads=[Bdb], writes=[Brd], out=rd[:], in_=db[:])
                              O("dve", "tensor_tensor", reads=[Bob, Brd], writes=[Bdo[pp][qt]], out=dil_out[:, pp, T4(qt)], in0=ob[:], in1=rd[:], op=ALU.mult)
                          chk(3)

                      chk(4)
                      for pp in range(2):
                          (wsq, wsk), Bwqk = load_unit([wcols(win_d, l, 2304 + 128 * pp, 128), wcols(win_d, l, 2560 + 128 * pp, 128)])
                          (wsv,), Bwv = load_unit([wcols(win_d, l, 2816 + 128 * pp, 128)])
                          for tt in range(4):
                              bank, Bb = PA.next()
                              proj_fm(wsq, tt, bank, Bb, Bwqk)
                              O("act", "copy", reads=[Bb], writes=[Bqk[0][tt]], out=qk[:, 0, T4(tt)], in_=bank[:])
                              bank, Bb = PA.next()
                              proj_fm(wsk, tt, bank, Bb, Bwqk)
                              O("act", "copy", reads=[Bb], writes=[Bqk[1][tt]], out=qk[:, 1, T4(tt)], in_=bank[:])
                              O("dve", "tensor_scalar", reads=[Bqk[1][tt]], writes=[Bqk[2][tt]], out=qk[:, 2, T4(tt)], in0=qk[:, 1, T4(tt)], scalar1=-0.125, scalar2=None, op0=ALU.mult)
                          for blk in range(16):
                              bank, Bb = PA.next()
                              proj_tm(wsv, blk, bank, Bb, Bwv, 128)
                              O("act", "copy", reads=[Bb], writes=[Bv[blk]], out=vb[:, blk, 0:2, :], in_=bank[:, 0:128].rearrange("p (h d) -> p h d", h=2))
                          for qt in range(4):
                              ob, Bob = PB.next()
                              tiles = [(kb, j) for kb in range(qt * 4 + 3, -1, -1) for j in range(2)]
                              stt_ = {}
                              accv = [None, None]

                              def stageA(kb, j, qt=qt, stt_=stt_, accv=accv):
                                  r0 = j * 64
                                  m = kb - 4 * qt
                                  zb, Bz = PA.next()
                                  O("pe", "matmul", reads=[Bqk[1][kb // 4], Bqk[0][qt]], writes=[Bz], out=zb[:],
                                    lhsT=qk[r0:r0 + 64, 1, kb * 128:(kb + 1) * 128], rhs=qk[r0:r0 + 64, 0, T4(qt)], start=True, stop=(m < 0))
                                  if m >= 0:
                                      mo = MOFF[3] + 384 - 128 * m
                                      O("pe", "matmul", reads=[Bconst], writes=[Bz], out=zb[:], lhsT=ident, rhs=cmask[:, mo:mo + 512], start=False, stop=True)
                                  et, Bet = ets.next()
                                  O("act", "activation", reads=[Bz], writes=[Bet], out=et[:], in_=zb[:], func=AF.Exp, scale=0.125)
                                  wt, Bwt = wts.next()
                                  O("act", "activation", reads=[Bet], writes=[Bwt], out=wt[:], in_=et[:], func=AF.Ln, bias=1.0)
                                  prev = accv[j]
                                  stt_[(kb, j)] = (wt, Bwt, prev)
                                  if kb > 0:
                                      a32, Ba32 = wacc32[j]
                                      if prev is None:
                                          O("dve", "tensor_copy", reads=[Bwt], writes=[Ba32], out=a32[:], in_=wt[:])
                                      else:
                                          O("dve", "tensor_tensor", reads=[Bwt, Ba32], writes=[Ba32], out=a32[:], in0=a32[:], in1=wt[:], op=ALU.add)
                                      ab, Bab = waccb[j].next()
                                      O("dve", "tensor_copy", reads=[Ba32], writes=[Bab], out=ab[:], in_=a32[:])
                                      accv[j] = (ab, Bab)

                              def stageB(kb, j, first, qt=qt, stt_=stt_, ob=ob, Bob=Bob):
                                  r0 = j * 64
                                  m = kb - 4 * qt
                                  wt, Bwt, prev = stt_.pop((kb, j))
                                  rb, Br = PA.next()
                                  O("pe", "matmul", reads=[Bwt, Bconst], writes=[Br], out=rb[:], lhsT=tri, rhs=wt[:], start=True, stop=False)
                                  if prev is not None:
                                      ab, Bab = prev
                                      O("pe", "matmul", reads=[Bab, Bconst], writes=[Br], out=rb[:], lhsT=ones, rhs=ab[:], start=False, stop=False)
                                  O("pe", "matmul", reads=[Bqk[2][kb // 4], Bqk[0][qt]], writes=[Br], out=rb[:],
                                    lhsT=qk[r0:r0 + 64, 2, kb * 128:(kb + 1) * 128], rhs=qk[r0:r0 + 64, 0, T4(qt)], start=False, stop=(m < 0))
                                  if m >= 0:
                                      mo = MOFF[3] + 384 - 128 * m
                                      O("pe", "matmul", reads=[Bconst], writes=[Br], out=rb[:], lhsT=negident, rhs=cmask[:, mo:mo + 512], start=False, stop=True)
                                  pt, Bpt = pts.next()
                                  O("act", "activation", reads=[Br], writes=[Bpt], out=pt[:], in_=rb[:], func=AF.Exp, scale=-1.0)
                                  O("pe", "matmul", reads=[Bpt, Bv[kb]], writes=[Bob], out=ob[r0:r0 + 64, :], lhsT=vb[:, kb, j, :], rhs=pt[:],
                                    start=first, stop=False, skip_group_check=True, tile_position=(0, r0))

                              LAG = 2
                              for i, (kb, j) in enumerate(tiles):
                                  stageA(kb, j)
                                  if i >= LAG:
                                      stageB(*tiles[i - LAG], first=(i - LAG) < 2)
                              for i in range(max(0, len(tiles) - LAG), len(tiles)):
                                  stageB(*tiles[i], first=i < 2)
                              O("act", "copy", reads=[Bob], writes=[Bso[pp][qt]], out=sb_out[:, pp, T4(qt)], in_=ob[:])
                  SC.barrier()

                  if stop == 5:
                      for c in range(2):
                          O("pool", "dma_start", reads=Bdo[c], chan=out_ch, out=dbg_d[:, c, :], in_=dil_out[:, c, :])
                          O("pool", "dma_start", reads=Bso[c], chan=out_ch, out=dbg_d[:, 2 + c, :], in_=sb_out[:, c, :])
                  chk(5)
                  with Safe() as es2:
                      def sb2(name, shape, dt):
                          return es2.enter_context(nc.sbuf_tensor(f"{name}_{l}", list(shape), dt))
                      mixed = sb2("mixed", [128, 8, S], BF16)
                      Bmx = [[Buf() for _ in range(4)] for _ in range(8)]
                      gts = Rot([(sb2(f"gt{i}", [128, 512], F32), Buf()) for i in range(4)])
                      mts = Rot([(sb2(f"mt{i}", [128, 512], F32), Buf()) for i in range(4)])
                      wpab = sb2("wpab", [128, 2, 2, D], BF16)
                      wpa, wpb = wpab[:, 0, :, :], wpab[:, 1, :, :]
                      Bwpa = Bwpb = Buf()
                      pch = Chan(es.enter_context(nc.semaphore(f"pch{l}")))
                      O("pool", "dma_start", writes=[Bwpa], chan=pch, out=wpa, in_=wpa_d[l].rearrange("(k p) n -> p k n", p=128))
                      O("pool", "dma_start", writes=[Bwpb], chan=pch, out=wpb, in_=wpb_d[l].rearrange("(k p) n -> p k n", p=128))
                      for dmc in range(8):
                          (wga, wgb), Bwg = load_unit([wcols(win_d, l, 3072 + dmc * 128, 128), wcols(win_d, l, 4096 + dmc * 128, 128)])
                          for tt in range(4):
                              res = []
                              for (wg, wp, Bwp, src, Bsrc, bcol) in ((wga, wpa, Bwpa, dil_out, Bdo, dmc), (wgb, wpb, Bwpb, sb_out, Bso, 8 + dmc)):
                                  gb_, Bgb = PA.next()
                                  proj_fm(wg, tt, gb_, Bgb, Bwg)
                                  gt, Bgt = gts.next()
                                  O("act", "activation", reads=[Bgb, Bconst], writes=[Bgt], out=gt[:], in_=gb_[:], func=AF.Sigmoid, bias=bgate[:, l, bcol:bcol + 1])
                                  yb, Byb = PA.next()
                                  for c in range(2):
                                      O("pe", "matmul", reads=[Bwp, Bsrc[c][tt]], writes=[Byb], out=yb[:], lhsT=wp[:, c, dmc * 128:(dmc + 1) * 128], rhs=src[:, c, T4(tt)], start=(c == 0), stop=(c == 1))
                                  mt, Bmt = mts.next()
                                  O("dve", "tensor_tensor", reads=[Byb, Bgt], writes=[Bmt], out=mt[:], in0=yb[:], in1=gt[:], op=ALU.mult)
                                  res.append((mt, Bmt))
                              (m1, Bm1), (m2, Bm2) = res
                              O("dve", "tensor_tensor", reads=[Bm1, Bm2], writes=[Bmx[dmc][tt]], out=mixed[:, dmc, T4(tt)], in0=m1[:], in1=m2[:], op=ALU.add)
                      for dp in range(4):
                          (wo,), Bwo = load_unit([wcols(wout_d, l, dp * 256, 256)])
                          for h2 in range(2):
                              dmo = dp * 2 + h2
                              for tt in range(4):
                                  bank, Bb = PA.next()
                                  for c in range(8):
                                      O("pe", "matmul", reads=[Bwo, Bmx[c][tt]], writes=[Bb], out=bank[:], lhsT=wo[:, c, h2 * 128:(h2 + 1) * 128], rhs=mixed[:, c, T4(tt)], start=(c == 0), stop=(c == 7))
                                  O("dve", "tensor_tensor", reads=[Bb, Bx[dmo][tt]], writes=[Bx[dmo][tt]], out=xT[:, dmo, T4(tt)], in0=bank[:], in1=xT[:, dmo, T4(tt)], op=ALU.add)
                  SC.barrier()

              chk(6)
              with Safe() as es3:
                  def sb3(name, shape, dt):
                      return es3.enter_context(nc.sbuf_tensor(f"{name}_{l}", list(shape), dt))
                  wk = {
                      "sq": Rot([(sb3(f"fsq{i}", [128, 512], BF16), Buf()) for i in range(2)]),
                      "rt": Rot([(sb3(f"frt{i}", [128, 512], F32), Buf()) for i in range(2)]),
                  }
                  gbuf = sb3("gbuf", [128, 6, S], BF16)
                  Bg = [[Buf() for _ in range(4)] for _ in range(6)]
                  asb = Rot([(sb3(f"asb{i}", [128, S + 2], F32), [Buf() for _ in range(5)]) for i in range(2)])
                  cts = Rot([(sb3(f"ct{i}", [128, S], F32), Buf()) for i in range(2)])
                  sts = Rot([(sb3(f"st{i}", [128, S], BF16), Buf()) for i in range(2)])
                  for (a_t, a_B) in asb.items:
                      O("dve", "memset", writes=[a_B[4]], ap=a_t[:, 0:2], constant=0.0)
                  norm_to_h(DEPTH + l, wk)
                  for (f0, nf) in FF_PARTS:
                      for fi in range(nf):
                          f = f0 + fi
                          (wa, wb), Bw = load_unit([wcols(wup_d, l, f * 128, 128), wcols(wup_d, l, D_FF + f * 128, 128)])
                          a_t, a_B = asb.next()
                          for tt in range(4):
                              bank, Bb = PA.next()
                              proj_fm(wa, tt, bank, Bb, Bw)
                              O("act", "copy", reads=[Bb], writes=[a_B[tt]], out=a_t[:, 2 + tt * 512:2 + (tt + 1) * 512], in_=bank[:])
                          ct, Bct = cts.next()
                          O("dve", "tensor_scalar", reads=a_B + [Bconst], writes=[Bct], out=ct[:], in0=a_t[:, 2:S + 2], scalar1=convw[:, l, 2, f:f + 1], scalar2=convb[:, l, f:f + 1], op0=ALU.mult, op1=ALU.add)
                          O("dve", "scalar_tensor_tensor", reads=a_B + [Bconst, Bct], writes=[Bct], out=ct[:], in0=a_t[:, 1:S + 1], scalar=convw[:, l, 1, f:f + 1], in1=ct[:], op0=ALU.mult, op1=ALU.add)
                          O("dve", "scalar_tensor_tensor", reads=a_B + [Bconst, Bct], writes=[Bct], out=ct[:], in0=a_t[:, 0:S], scalar=convw[:, l, 0, f:f + 1], in1=ct[:], op0=ALU.mult, op1=ALU.add)
                          stt, Bst = sts.next()
                          O("act", "activation", reads=[Bct], writes=[Bst], out=stt[:], in_=ct[:], func=AF.Silu)
                          for tt in range(4):
                              bank, Bb = PA.next()
                              proj_fm(wb, tt, bank, Bb, Bw)
                              O("dve", "tensor_tensor", reads=[Bb, Bst], writes=[Bg[fi][tt]], out=gbuf[:, fi, T4(tt)], in0=bank[:], in1=stt[:, T4(tt)], op=ALU.mult)
                      for dp in range(4):
                          (wd,), Bwd = load_unit([(wdn_d[l][f0 * 128:(f0 + nf) * 128, dp * 256:(dp + 1) * 256].rearrange("(k p) n -> p k n", p=128), nf, 256, Bgath[("w_down", l)])])
                          for h2 in range(2):
                              dmo = dp * 2 + h2
                              for tt in range(4):
                                  bank, Bb = PA.next()
                                  for fi in range(nf):
                                      O("pe", "matmul", reads=[Bwd, Bg[fi][tt]], writes=[Bb], out=bank[:], lhsT=wd[:, fi, h2 * 128:(h2 + 1) * 128], rhs=gbuf[:, fi, T4(tt)], start=(fi == 0), stop=(fi == nf - 1))
                                  O("dve", "tensor_tensor", reads=[Bb, Bx[dmo][tt]], writes=[Bx[dmo][tt]], out=xT[:, dmo, T4(tt)], in0=bank[:], in1=xT[:, dmo, T4(tt)], op=ALU.add)
              SC.barrier()

        try:
            layers()
        except StopBuild:
            SC.barrier()

        with contextlib.ExitStack() as es4:
            def sb4(name, shape, dt):
                return es4.enter_context(nc.sbuf_tensor(name, list(shape), dt))
            wk = {
                "sq": Rot([(sb4(f"lsq{i}", [128, 512], BF16), Buf()) for i in range(2)]),
                "rt": Rot([(sb4(f"lrt{i}", [128, 512], F32), Buf()) for i in range(2)]),
            }
            outs = Rot([(sb4(f"ot{i}", [128, 8, 512], F32), Buf()) for i in range(2)])
            outT_v = outT_d.rearrange("(c p) t -> p c t", p=128)
            for tt, rt, Brt in rmsnorm(wk):
                ot, Bot = outs.next()
                for c in range(8):
                    O("dve", "scalar_tensor_tensor", reads=[Bx[c][tt], Brt, Bconst], writes=[Bot],
                      out=ot[:, c, :], in0=xT[:, c, T4(tt)], scalar=gains[:, 2 * DEPTH, c:c + 1], in1=rt[:], op0=ALU.mult, op1=ALU.mult)
                O("sp", "dma_start", reads=[Bot], chan=out_ch, out=outT_v[:, :, T4(tt)], in_=ot[:])

        SC.finalize(lambda name: es.enter_context(nc.semaphore(name)))
        with nc.Block() as block:
            @block.tensor
            def _(e):
                SC.run("pe", e)

            @block.scalar
            def _(e):
                SC.run("act", e)

            @block.vector
            def _(e):
                SC.run("dve", e)

            @block.gpsimd
            def _(e):
                SC.run("pool", e)

            @block.sync
            def _(e):
                SC.run("sp", e)
                e.wait_ge(out_ch.sem, out_ch.count)
    return nc


_NC_CACHE = {}


def kernel(x, positions, norm_mix, w_in, b_gate, w_proj_a, w_proj_b, w_out,
           norm_ffn, w_up, conv_w, conv_b, w_down, norm_final, _n_layers=DEPTH, _n_cores=NC8, _stop=99):
    f = lambda a: np.ascontiguousarray(np.asarray(a, dtype=np.float32))
    x = np.asarray(x, dtype=np.float32)
    positions = np.asarray(positions).astype(np.int32)
    def pc(a, inner):
        a = f(a)
        lead = a.shape[:-1]
        a = a.reshape(lead + (inner, 128))
        a = np.moveaxis(a, -1, 0)
        return np.ascontiguousarray(a.reshape(128, -1))
    gains_h = np.concatenate([pc(norm_mix, 8), pc(norm_ffn, 8), pc(f(norm_final).reshape(1, D), 8)], axis=1)
    shared = {
        "gains_h": np.ascontiguousarray(gains_h), "bgate_h": pc(b_gate, 16),
        "convw_h": pc(conv_w, NFC), "convb_h": pc(conv_b, NFC),
    }
    wfull = {"w_in": f(w_in), "w_proj_a": f(w_proj_a), "w_proj_b": f(w_proj_b), "w_out": f(w_out), "w_up": f(w_up), "w_down": f(w_down)}
    for nm, w in wfull.items():
        for l in range(_n_layers):
            shared[f"{nm}_{l}"] = np.ascontiguousarray(w[l])
    shared.update(_consts())
    if (_n_layers, _stop) not in _NC_CACHE:
        _NC_CACHE[(_n_layers, _stop)] = build(_n_layers, _stop)
    nc = _NC_CACHE[(_n_layers, _stop)]
    in_maps = []
    for c in range(_n_cores):
        m = dict(shared)
        m["xT"] = np.ascontiguousarray(x[c].T)
        m["pos"] = np.ascontiguousarray(positions[c].reshape(1, S))

        in_maps.append(m)
    res = run_bass_kernel_spmd(nc, in_maps, core_ids=list(range(_n_cores)))
    global _DBG
    _DBG = [r.get("dbg") for r in res.results]
    out = np.stack([np.asarray(r["outT"]).T for r in res.results], axis=0)
    return np.ascontiguousarray(out.astype(np.float32))
```

```python
import contextlib
import math

import numpy as np
import ml_dtypes

import concourse.bass as bass
import concourse.mybir as mybir
from concourse.bass_utils import run_bass_kernel_spmd

F32 = mybir.dt.float32
BF16 = mybir.dt.bfloat16
I32 = mybir.dt.int32
AF = mybir.ActivationFunctionType
ALU = mybir.AluOpType

D = 1024
S = 2048
DEPTH = 4
NC8 = 8
IN_COLS = 5120
D_FF = 2816
NFC = 22
EPS = 1e-5
NEG = -30000.0
DIL = ((128, 1), (512, 4), (2048, 16))
FF_PARTS = ((0, 6), (6, 6), (12, 5), (17, 5))
SLOT = 2048
NSLOT = 3


class Buf:
    __slots__ = ("name", "w", "r")

    def __init__(self, name=""):
        self.name = name
        self.w = None
        self.r = []


class Chan:
    def __init__(self, sem):
        self.sem = sem
        self.count = 0


class Op:
    __slots__ = ("eng", "fn", "deps", "ticket", "inc", "chan", "chan_val", "waits")


class Sched:
    ENGS = ("pe", "act", "dve", "pool", "sp")

    def __init__(self):
        self.ops = {e: [] for e in self.ENGS}
        self.all = []
        self.extra = []

    def barrier(self):
        self.extra = [self.ops[e][-1] for e in self.ENGS if self.ops[e]]

    def op(self, eng, meth, reads=(), writes=(), chan=None, nobar=False, **kw):
        o = Op()
        o.eng, o.fn, o.chan = eng, (meth, kw), chan
        o.chan_val = None
        o.inc = False
        o.ticket = None
        deps = [] if nobar else list(self.extra)
        for b in reads:
            if b.w is not None:
                deps.append(b.w)
        for b in writes:
            if b.w is not None:
                deps.append(b.w)
            last = {}
            for r in b.r:
                last[(r.eng, id(r.chan) if r.chan is not None else 0)] = r
            deps.extend(last.values())
        o.deps = deps
        for b in reads:
            b.r.append(o)
        for b in writes:
            b.w = o
            b.r = []
        if chan is not None:
            chan.count += 16
            o.chan_val = chan.count
        self.ops[eng].append(o)
        self.all.append(o)
        return o

    EPOCH = 1500

    def finalize(self, alloc):
        for o in self.all:
            for d in o.deps:
                if d is o or d.chan is not None:
                    continue
                if d.eng == "pe" and o.eng == "pe":
                    continue
                d.inc = True
        sems = {}
        for e in self.ENGS:
            t = 0
            for o in self.ops[e]:
                if o.chan is None and o.inc:
                    t += 1
                    o.ticket = t
            sems[e] = [alloc(f"s_{e}_{i}") for i in range((t + self.EPOCH - 1) // self.EPOCH)]
        E = self.EPOCH
        for e in self.ENGS:
            seen = {}
            for o in self.ops[e]:
                need = {}
                for d in o.deps:
                    if d is o:
                        continue
                    if d.chan is not None:
                        key, sem, val, gval = ("c", id(d.chan)), d.chan.sem, d.chan_val, d.chan_val
                    else:
                        if d.eng == "pe" and o.eng == "pe":
                            continue
                        key, gval = ("e", d.eng), d.ticket
                        sem, val = sems[d.eng][(gval - 1) // E], (gval - 1) % E + 1
                    if seen.get(key, 0) >= gval:
                        continue
                    if key not in need or need[key][2] < gval:
                        need[key] = (sem, val, gval)
                o.waits = [(sem, val) for (sem, val, _) in need.values()]
                for key, (sem, val, gval) in need.items():
                    seen[key] = gval
                o.deps = None
        self.sems = sems

    def run(self, e, eng):
        sems = self.sems
        E = self.EPOCH
        for o in self.ops[e]:
            for sem, val in o.waits:
                eng.wait_ge(sem, val)
            meth, kw = o.fn
            ins = getattr(eng, meth)(**kw)
            if o.chan is not None:
                ins.then_inc(o.chan.sem, 16)
            elif o.inc:
                ins.then_inc(sems[e][(o.ticket - 1) // E], 1)


class Rot:
    def __init__(self, items):
        self.items = items
        self.i = 0

    def next(self):
        it = self.items[self.i % len(self.items)]
        self.i += 1
        return it


def _consts():
    c = {}
    eye = np.eye(128, dtype=np.float32)
    perm = np.zeros((128, 128), np.float32)
    for hb in (0, 64):
        for i in range(8):
            perm[hb + 8 + i, hb + i] = 1.0
            perm[hb + i, hb + 8 + i] = 1.0
    k = np.arange(128)[:, None]
    m = np.arange(128)[None, :]
    tri = (k >= m).astype(np.float32)
    mats = np.stack([eye, -eye, perm, tri, np.ones((128, 128), np.float32)], axis=1)
    c["cmats"] = np.ascontiguousarray(mats.reshape(128, 5 * 128))
    p = np.arange(128)[:, None]
    msk = []
    for (W, dil), width in zip(DIL, (256, 640, 640)):
        x = np.arange(width)[None, :]
        dlt = x - p
        valid = (dlt >= 0) & (dlt <= W) & (dlt % dil == 0)
        msk.append(np.where(valid, 0.0, NEG).astype(np.float32))
    x = np.arange(896)[None, :]
    msk.append(np.where(x - 384 > p, 0.0, NEG).astype(np.float32))
    c["cmask"] = np.ascontiguousarray(np.concatenate(msk, axis=1))
    invf = (500000.0 ** (-np.arange(0, 16, 2, dtype=np.float32) / 16.0)).astype(np.float32)
    col = np.zeros((128, 2), np.float32)
    for hb in (0, 64):
        col[hb:hb + 8, 0] = invf
        col[hb + 8:hb + 16, 0] = invf
        col[hb:hb + 8, 1] = -1.0
        col[hb + 8:hb + 16, 1] = 1.0
    c["ccol"] = col
    return c


MOFF = (0, 256, 896, 1536)
MW = 2432


class StopBuild(Exception):
    pass


class Safe(contextlib.ExitStack):
    def __exit__(self, et, ev, tb):
        super().__exit__(None, None, None)
        return False


def build(n_layers=DEPTH, stop=99):
    nc = bass.Bass("TRN2", target_bir_lowering=False)

    def din(name, shape, dt=F32):
        return nc.dram_tensor(name, list(shape), dt, kind="ExternalInput").ap()

    xT_d = din("xT", [D, S])
    pos_d = din("pos", [1, S], I32)
    gains_d = din("gains_h", [128, (2 * DEPTH + 1) * 8])
    bgate_d = din("bgate_h", [128, DEPTH * 16])
    convw_d = din("convw_h", [128, DEPTH * 3 * NFC])
    convb_d = din("convb_h", [128, DEPTH * NFC])
    WSPEC = (("w_in", D, IN_COLS), ("w_proj_a", 256, D), ("w_proj_b", 256, D), ("w_out", D, D), ("w_up", D, 2 * D_FF), ("w_down", D_FF, D))
    gath_d = {(nm, l): din(f"{nm}_{l}", [rows, cols]) for nm, rows, cols in WSPEC for l in range(n_layers)}

    class _G:
        def __init__(self, nm):
            self.nm = nm

        def __getitem__(self, l):
            return gath_d[(self.nm, l)]
    win_d, wpa_d, wpb_d, wout_d, wup_d, wdn_d = (_G(nm) for nm, _, _ in WSPEC)
    Bgath = {k: Buf() for k in gath_d}
    cmats_d = din("cmats", [128, 640])
    cmask_d = din("cmask", [128, MW])
    ccol_d = din("ccol", [128, 2])
    outT_d = nc.dram_tensor("outT", [D, S], F32, kind="ExternalOutput").ap()
    dbg_d = nc.dram_tensor("dbg", [128, 4, S], F32, kind="ExternalOutput").ap() if stop < 99 else None

    SC = Sched()
    O = SC.op
    es = contextlib.ExitStack()
    with es:
        def sb(name, shape, dt):
            return es.enter_context(nc.sbuf_tensor(name, list(shape), dt))

        xT = sb("xT_sb", [128, 8, S], F32)
        hT = sb("hT_sb", [128, 8, S], BF16)
        ropeC = sb("ropeC", [128, S], BF16)
        ropeS = sb("ropeS", [128, S], BF16)
        cmats = sb("cmats_sb", [128, 5, 128], BF16)
        cmask = sb("cmask_sb", [128, MW], BF16)
        ccol = sb("ccol_sb", [128, 2], F32)
        gains = sb("gains", [128, 2 * DEPTH + 1, 8], F32)
        bgate = sb("bgate", [128, DEPTH, 16], F32)
        convw = sb("convw", [128, DEPTH, 3, NFC], F32)
        convb = sb("convb", [128, DEPTH, NFC], F32)
        ring = sb("ring", [128, NSLOT, SLOT], BF16)
        ident, negident, perm, tri, ones = (cmats[:, i, :] for i in range(5))

        psum = [es.enter_context(nc.psum_tensor(f"ps{i}", [128, 512], F32)) for i in range(8)]
        PA = Rot([(psum[i], Buf(f"ps{i}")) for i in range(5)])
        PB = Rot([(psum[i], Buf(f"ps{i}")) for i in range(5, 8)])

        slot_ch = [Chan(es.enter_context(nc.semaphore(f"slot{i}"))) for i in range(NSLOT)]
        slot_buf = [Buf(f"slot{i}") for i in range(NSLOT)]
        x_ch = [Chan(es.enter_context(nc.semaphore(f"xch{i}"))) for i in range(8)]
        init_ch = Chan(es.enter_context(nc.semaphore("initch")))
        out_ch = Chan(es.enter_context(nc.semaphore("outch")))
        bnc_ch = [Chan(es.enter_context(nc.semaphore(f"bnc{i}"))) for i in range(DEPTH)]

        Bx = [[Buf() for _ in range(4)] for _ in range(8)]
        Bh = [[Buf() for _ in range(4)] for _ in range(8)]
        Bconst = Buf("const")
        Brope = Buf("rope")

        def T4(t):
            return slice(t * 512, (t + 1) * 512)

        slot_i = [0]

        def load_unit(pieces):
            si = slot_i[0] % NSLOT
            slot_i[0] += 1
            b = slot_buf[si]
            lastr = {}
            for r in b.r:
                lastr[r.eng] = r
            old = ([b.w] if b.w is not None else []) + list(lastr.values())
            off = 0
            views = []
            last = None
            for (src, k, n, bsrc) in pieces:
                dst = ring[:, si, off:off + k * n].rearrange("p (k n) -> p k n", k=k)
                tmp = Buf()
                tmp.r = list(old)
                last = O("pool", "dma_start", reads=[bsrc], writes=[tmp], chan=slot_ch[si], nobar=True, out=dst, in_=src)
                views.append(dst)
                off += k * n
            assert off <= SLOT
            b.w = last
            b.r = []
            return views, b

        def wcols(wd, l, c0, n):
            return (wd[l].rearrange("(k p) n -> p k n", p=128)[:, :, c0:c0 + n], 8, n, Bgath[(wd.nm, l)])

        def vec8(src):
            return src.rearrange("(c p) -> p c", p=128)

        O("pool", "dma_start", writes=[Bconst], chan=init_ch, out=cmats[:].rearrange("p a b -> p (a b)"), in_=cmats_d)
        O("pool", "dma_start", writes=[Bconst], chan=init_ch, out=cmask[:], in_=cmask_d)
        O("sp", "dma_start", writes=[Bconst], chan=init_ch, out=ccol[:], in_=ccol_d)
        O("sp", "dma_start", writes=[Bconst], chan=init_ch, out=gains[:].rearrange("p a b -> p (a b)"), in_=gains_d)
        O("sp", "dma_start", writes=[Bconst], chan=init_ch, out=bgate[:].rearrange("p a b -> p (a b)"), in_=bgate_d)
        O("sp", "dma_start", writes=[Bconst], chan=init_ch, out=convw[:].rearrange("p a b c -> p (a b c)"), in_=convw_d)
        O("sp", "dma_start", writes=[Bconst], chan=init_ch, out=convb[:].rearrange("p a b -> p (a b)"), in_=convb_d)
        for c in range(8):
            O("sp", "dma_start", writes=Bx[c], chan=x_ch[c], out=xT[:, c, :], in_=xT_d[c * 128:(c + 1) * 128, :])

        with contextlib.ExitStack() as es0:
            posi = es0.enter_context(nc.sbuf_tensor("posi", [128, S], I32))
            ang = es0.enter_context(nc.sbuf_tensor("ang", [128, S], F32))
            ta = es0.enter_context(nc.sbuf_tensor("ta", [128, S], F32))
            tb = es0.enter_context(nc.sbuf_tensor("tb", [128, S], F32))
            ki = es0.enter_context(nc.sbuf_tensor("ki", [128, S], I32))
            Bp, Bang, Bta, Btb, Bki = Buf(), Buf(), Buf(), Buf(), Buf()
            O("sp", "dma_start", writes=[Bp], chan=init_ch, out=posi[:], in_=pos_d[0, :].partition_broadcast(128))
            O("dve", "tensor_copy", reads=[Bp], writes=[Bang], out=ang[:], in_=posi[:])
            O("dve", "tensor_scalar", reads=[Bang, Bconst], writes=[Bang], out=ang[:], in0=ang[:], scalar1=ccol[:, 0:1], scalar2=None, op0=ALU.mult)

            def sin_table(shift, dst, use_sign):
                O("dve", "tensor_scalar", reads=[Bang], writes=[Bta], out=ta[:], in0=ang[:], scalar1=shift, scalar2=1.0 / (2 * math.pi), op0=ALU.add, op1=ALU.mult)
                O("dve", "tensor_copy", reads=[Bta], writes=[Bki], out=ki[:], in_=ta[:])
                O("dve", "tensor_copy", reads=[Bki], writes=[Btb], out=tb[:], in_=ki[:])
                O("dve", "tensor_scalar", reads=[Bang], writes=[Bta], out=ta[:], in0=ang[:], scalar1=shift, scalar2=None, op0=ALU.add)
                O("dve", "scalar_tensor_tensor", reads=[Btb, Bta], writes=[Bta], out=ta[:], in0=tb[:], scalar=-2 * math.pi, in1=ta[:], op0=ALU.mult, op1=ALU.add)
                O("dve", "tensor_scalar", reads=[Bta], writes=[Btb], out=tb[:], in0=ta[:], scalar1=math.pi, scalar2=-2 * math.pi, op0=ALU.is_gt, op1=ALU.mult)
                O("dve", "tensor_tensor", reads=[Bta, Btb], writes=[Bta], out=ta[:], in0=ta[:], in1=tb[:], op=ALU.add)
                O("dve", "tensor_scalar", reads=[Bta], writes=[Bta], out=ta[:], in0=ta[:], scalar1=-3.1415925, scalar2=3.1415925, op0=ALU.max, op1=ALU.min)
                if use_sign:
                    O("act", "activation", reads=[Bta], writes=[Btb], out=tb[:], in_=ta[:], func=AF.Sin)
                    O("dve", "tensor_scalar", reads=[Btb, Bconst], writes=[Brope], out=dst[:], in0=tb[:], scalar1=ccol[:, 1:2], scalar2=None, op0=ALU.mult)
                else:
                    O("act", "activation", reads=[Bta], writes=[Brope], out=dst[:], in_=ta[:], func=AF.Sin)

            sin_table(0.0, ropeS, True)
            sin_table(math.pi / 2, ropeC, False)
        SC.barrier()

        def rmsnorm(wk):
            res = []
            for tt in range(4):
                bank, Bb = PA.next()
                for c in range(8):
                    sq, Bsq = wk["sq"].next()
                    O("act", "activation", reads=[Bx[c][tt]], writes=[Bsq], out=sq[:], in_=xT[:, c, T4(tt)], func=AF.Square)
                    O("pe", "matmul", reads=[Bsq, Bconst], writes=[Bb], out=bank[:], lhsT=ones, rhs=sq[:], start=(c == 0), stop=(c == 7))
                rt, Brt = wk["rt"].next()
                O("act", "activation", reads=[Bb], writes=[Brt], out=rt[:], in_=bank[:], func=AF.Sqrt, scale=1.0 / D, bias=EPS)
                O("dve", "reciprocal", reads=[Brt], writes=[Brt], out=rt[:], in_=rt[:])
                yield tt, rt, Brt

        def norm_to_h(gidx, wk):
            for tt, rt, Brt in rmsnorm(wk):
                for c in range(8):
                    O("dve", "scalar_tensor_tensor", reads=[Bx[c][tt], Brt, Bconst], writes=[Bh[c][tt]],
                      out=hT[:, c, T4(tt)], in0=xT[:, c, T4(tt)], scalar=gains[:, gidx, c:c + 1], in1=rt[:], op0=ALU.mult, op1=ALU.mult)

        def proj_fm(wview, tt, bank, Bb, Bw):
            for kc in range(8):
                O("pe", "matmul", reads=[Bw, Bh[kc][tt]], writes=[Bb], out=bank[:], lhsT=wview[:, kc, :], rhs=hT[:, kc, T4(tt)], start=(kc == 0), stop=(kc == 7))

        def proj_tm(wview, blk, bank, Bb, Bw, n):
            for kc in range(8):
                O("pe", "matmul", reads=[Bw, Bh[kc][blk // 4]], writes=[Bb], out=bank[:, 0:n], lhsT=hT[:, kc, blk * 128:(blk + 1) * 128], rhs=wview[:, kc, :], start=(kc == 0), stop=(kc == 7))

        def chk(stage):
            if stop <= stage:
                raise StopBuild()

        def layers():
          for l in range(n_layers):
            chk(0)
            for si_ in range(NSLOT):
                slot_ch[si_] = Chan(es.enter_context(nc.semaphore(f"slot{si_}_{l}")))
            if True:
              with Safe() as esm:
                  dil_out = esm.enter_context(nc.sbuf_tensor(f"dil_out_{l}", [128, 2, S], BF16))
                  sb_out = esm.enter_context(nc.sbuf_tensor(f"sb_out_{l}", [128, 2, S], BF16))
                  Bdo = [[Buf() for _ in range(4)] for _ in range(2)]
                  Bso = [[Buf() for _ in range(4)] for _ in range(2)]

                  with Safe() as es1:
                      def sb1(name, shape, dt):
                          return es1.enter_context(nc.sbuf_tensor(f"{name}_{l}", list(shape), dt))
                      qk = sb1("qk", [128, 6, S], BF16)
                      vb = sb1("vb", [128, 16, 6, 64], BF16)
                      Bqk = [[Buf() for _ in range(4)] for _ in range(6)]
                      Bv = [Buf() for _ in range(16)]
                      f32p = Rot([(sb1(f"f32p{i}", [128, 512], F32), Buf()) for i in range(3)])
                      wk = {"sq": Rot([(sb1(f"sq{i}", [128, 512], BF16), Buf()) for i in range(2)]), "rt": f32p}
                      qraw = Rot([(sb1(f"qraw{i}", [128, 512], BF16), Buf()) for i in range(2)])
                      pts = Rot([(sb1(f"pt{i}", [128, 512], BF16), Buf()) for i in range(3)])
                      ets = Rot([(sb1(f"et{i}", [128, 512], F32), Buf()) for i in range(2)])
                      wts = Rot([(sb1(f"wt{i}", [128, 512], BF16), Buf()) for i in range(3)])
                      wacc32 = [(sb1(f"wacc32_{j}", [128, 512], F32), Buf()) for j in range(2)]
                      waccb = [Rot([(sb1(f"waccb{j}_{i}", [128, 512], BF16), Buf()) for i in range(3)]) for j in range(2)]

                      norm_to_h(l, wk)
                      chk(1)

                      for pp in range(2):
                          rope_pend = []

                          def ropeB(qr, Bqr, ci, tt):
                              bank2, Bb2 = PA.next()
                              O("pe", "matmul", reads=[Bqr, Bconst], writes=[Bb2], out=bank2[:], lhsT=perm, rhs=qr[:], start=True, stop=True)
                              t1, Bt1 = f32p.next()
                              t2, Bt2 = f32p.next()
                              O("dve", "tensor_tensor", reads=[Bqr, Brope], writes=[Bt1], out=t1[:], in0=qr[:], in1=ropeC[:, T4(tt)], op=ALU.mult)
                              O("dve", "tensor_tensor", reads=[Bb2, Brope], writes=[Bt2], out=t2[:], in0=bank2[:], in1=ropeS[:, T4(tt)], op=ALU.mult)
                              O("dve", "tensor_tensor", reads=[Bt1, Bt2], writes=[Bqk[ci][tt]], out=qk[:, ci, T4(tt)], in0=t1[:], in1=t2[:], op=ALU.add)

                          for g in range(3):
                              (wq, wkk), Bw = load_unit([wcols(win_d, l, g * 256 + 128 * pp, 128), wcols(win_d, l, 768 + g * 256 + 128 * pp, 128)])
                              for ci, wv_ in ((g, wq), (3 + g, wkk)):
                                  for tt in range(4):
                                      bank, Bb = PA.next()
                                      proj_fm(wv_, tt, bank, Bb, Bw)
                                      qr, Bqr = qraw.next()
                                      O("act", "copy", reads=[Bb], writes=[Bqr], out=qr[:], in_=bank[:])
                                      if rope_pend:
                                          ropeB(*rope_pend.pop())
                                      rope_pend.append((qr, Bqr, ci, tt))
                          ropeB(*rope_pend.pop())
                          vv01, Bw01 = load_unit([wcols(win_d, l, 1536 + g * 256 + 128 * pp, 128) for g in range(2)])
                          vv2, Bw2 = load_unit([wcols(win_d, l, 1536 + 512 + 128 * pp, 128)])
                          vviews = [(vv01[0], Bw01), (vv01[1], Bw01), (vv2[0], Bw2)]
                          for blk in range(16):
                              for g in range(3):
                                  bank, Bb = PA.next()
                                  proj_tm(vviews[g][0], blk, bank, Bb, vviews[g][1], 128)
                                  O("act", "copy", reads=[Bb], writes=[Bv[blk]], out=vb[:, blk, 2 * g:2 * g + 2, :], in_=bank[:, 0:128].rearrange("p (h d) -> p h d", h=2))
                          chk(2)
                          for qt in range(4):
                              ob, Bob = PB.next()
                              db, Bdb = PB.next()
                              first = [True, True]
                              dtiles = []
                              for g, (W, dil) in enumerate(DIL):
                                  kb_lo = max(0, (qt * 512 - W) // 128)
                                  kb_hi = qt * 4 + 3
                                  for kb in range(kb_lo, kb_hi + 1):
                                      qs = max(qt * 512, kb * 128)
                                      qe = min((qt + 1) * 512, kb * 128 + W + 128)
                                      n = qe - qs
                                      if n <= 0:
                                          continue
                                      x0 = qs - kb * 128
                                      if g == 2 and x0 >= 128:
                                          x0 = 128
                                      for j in range(2):
                                          dtiles.append((g, kb, j, qs, qe, n, MOFF[g] + x0, qs - qt * 512))
                              dst_ = {}

                              def dA(i):
                                  g, kb, j, qs, qe, n, mo, c0 = dtiles[i]
                                  r0 = j * 64
                                  bank, Bb = PA.next()
                                  O("pe", "matmul", reads=[Bqk[3 + g][kb // 4], Bqk[g][qt]], writes=[Bb], out=bank[:, 0:n],
                                    lhsT=qk[r0:r0 + 64, 3 + g, kb * 128:(kb + 1) * 128], rhs=qk[r0:r0 + 64, g, qs:qe], start=True, stop=False)
                                  O("pe", "matmul", reads=[Bconst], writes=[Bb], out=bank[:, 0:n], lhsT=ident, rhs=cmask[:, mo:mo + n], start=False, stop=True)
                                  pt, Bpt = pts.next()
                                  O("act", "activation", reads=[Bb], writes=[Bpt], out=pt[:, 0:n], in_=bank[:, 0:n], func=AF.Exp, scale=0.125)
                                  dst_[i] = (pt, Bpt)

                              def dB(i):
                                  g, kb, j, qs, qe, n, mo, c0 = dtiles[i]
                                  r0 = j * 64
                                  pt, Bpt = dst_.pop(i)
                                  st = first[j]
                                  first[j] = False
                                  O("pe", "matmul", reads=[Bpt, Bv[kb]], writes=[Bob], out=ob[r0:r0 + 64, c0:c0 + n], lhsT=vb[:, kb, 2 * g + j, :], rhs=pt[:, 0:n],
                                    start=st, stop=False, skip_group_check=True, tile_position=(0, r0))
                                  O("pe", "matmul", reads=[Bpt, Bconst], writes=[Bdb], out=db[r0:r0 + 64, c0:c0 + n], lhsT=ones[:, 0:64], rhs=pt[:, 0:n],
                                    start=st, stop=False, skip_group_check=True, tile_position=(0, r0))

                              DLAG = 2
                              for i in range(len(dtiles)):
                                  dA(i)
                                  if i >= DLAG:
                                      dB(i - DLAG)
                              for i in range(max(0, len(dtiles) - DLAG), len(dtiles)):
                                  dB(i)
                              rd, Brd = f32p.next()
                              O("dve", "reciprocal", reads=[Bdb], writes=[Brd], out=rd[:], in_=db[:])
                              O("dve", "tensor_tensor", reads=[Bob, Brd], writes=[Bdo[pp][qt]], out=dil_out[:, pp, T4(qt)], in0=ob[:], in1=rd[:], op=ALU.mult)
                          chk(3)

                      chk(4)
                      for pp in range(2):
                          (wsq, wsk), Bwqk = load_unit([wcols(win_d, l, 2304 + 128 * pp, 128), wcols(win_d, l, 2560 + 128 * pp, 128)])
                          (wsv,), Bwv = load_unit([wcols(win_d, l, 2816 + 128 * pp, 128)])
                          for tt in range(4):
                              bank, Bb = PA.next()
                              proj_fm(wsq, tt, bank, Bb, Bwqk)
                              O("act", "copy", reads=[Bb], writes=[Bqk[0][tt]], out=qk[:, 0, T4(tt)], in_=bank[:])
                              bank, Bb = PA.next()
                              proj_fm(wsk, tt, bank, Bb, Bwqk)
                              O("act", "copy", reads=[Bb], writes=[Bqk[1][tt]], out=qk[:, 1, T4(tt)], in_=bank[:])
                              O("dve", "tensor_scalar", reads=[Bqk[1][tt]], writes=[Bqk[2][tt]], out=qk[:, 2, T4(tt)], in0=qk[:, 1, T4(tt)], scalar1=-0.125, scalar2=None, op0=ALU.mult)
                          for blk in range(16):
                              bank, Bb = PA.next()
                              proj_tm(wsv, blk, bank, Bb, Bwv, 128)
                              O("act", "copy", reads=[Bb], writes=[Bv[blk]], out=vb[:, blk, 0:2, :], in_=bank[:, 0:128].rearrange("p (h d) -> p h d", h=2))
                          for qt in range(4):
                              ob, Bob = PB.next()
                              tiles = [(kb, j) for kb in range(qt * 4 + 3, -1, -1) for j in range(2)]
                              stt_ = {}
                              accv = [None, None]

                              def stageA(kb, j, qt=qt, stt_=stt_, accv=accv):
                                  r0 = j * 64
                                  m = kb - 4 * qt
                                  zb, Bz = PA.next()
                                  O("pe", "matmul", reads=[Bqk[1][kb // 4], Bqk[0][qt]], writes=[Bz], out=zb[:],
                                    lhsT=qk[r0:r0 + 64, 1, kb * 128:(kb + 1) * 128], rhs=qk[r0:r0 + 64, 0, T4(qt)], start=True, stop=(m < 0))
                                  if m >= 0:
                                      mo = MOFF[3] + 384 - 128 * m
                                      O("pe", "matmul", reads=[Bconst], writes=[Bz], out=zb[:], lhsT=ident, rhs=cmask[:, mo:mo + 512], start=False, stop=True)
                                  et, Bet = ets.next()
                                  O("act", "activation", reads=[Bz], writes=[Bet], out=et[:], in_=zb[:], func=AF.Exp, scale=0.125)
                                  wt, Bwt = wts.next()
                                  O("act", "activation", reads=[Bet], writes=[Bwt], out=wt[:], in_=et[:], func=AF.Ln, bias=1.0)
                                  prev = accv[j]
                                  stt_[(kb, j)] = (wt, Bwt, prev)
                                  if kb > 0:
                                      a32, Ba32 = wacc32[j]
                                      if prev is None:
                                          O("dve", "tensor_copy", reads=[Bwt], writes=[Ba32], out=a32[:], in_=wt[:])
                                      else:
                                          O("dve", "tensor_tensor", reads=[Bwt, Ba32], writes=[Ba32], out=a32[:], in0=a32[:], in1=wt[:], op=ALU.add)
                                      ab, Bab = waccb[j].next()
                                      O("dve", "tensor_copy", reads=[Ba32], writes=[Bab], out=ab[:], in_=a32[:])
                                      accv[j] = (ab, Bab)

                              def stageB(kb, j, first, qt=qt, stt_=stt_, ob=ob, Bob=Bob):
                                  r0 = j * 64
                                  m = kb - 4 * qt
                                  wt, Bwt, prev = stt_.pop((kb, j))
                                  rb, Br = PA.next()
                                  O("pe", "matmul", reads=[Bwt, Bconst], writes=[Br], out=rb[:], lhsT=tri, rhs=wt[:], start=True, stop=False)
                                  if prev is not None:
                                      ab, Bab = prev
                                      O("pe", "matmul", reads=[Bab, Bconst], writes=[Br], out=rb[:], lhsT=ones, rhs=ab[:], start=False, stop=False)
                                  O("pe", "matmul", reads=[Bqk[2][kb // 4], Bqk[0][qt]], writes=[Br], out=rb[:],
                                    lhsT=qk[r0:r0 + 64, 2, kb * 128:(kb + 1) * 128], rhs=qk[r0:r0 + 64, 0, T4(qt)], start=False, stop=(m < 0))
                                  if m >= 0:
                                      mo = MOFF[3] + 384 - 128 * m
                                      O("pe", "matmul", reads=[Bconst], writes=[Br], out=rb[:], lhsT=negident, rhs=cmask[:, mo:mo + 512], start=False, stop=True)
                                  pt, Bpt = pts.next()
                                  O("act", "activation", reads=[Br], writes=[Bpt], out=pt[:], in_=rb[:], func=AF.Exp, scale=-1.0)
                                  O("pe", "matmul", reads=[Bpt, Bv[kb]], writes=[Bob], out=ob[r0:r0 + 64, :], lhsT=vb[:, kb, j, :], rhs=pt[:],
                                    start=first, stop=False, skip_group_check=True, tile_position=(0, r0))

                              LAG = 2
                              for i, (kb, j) in enumerate(tiles):
                                  stageA(kb, j)
                                  if i >= LAG:
                                      stageB(*tiles[i - LAG], first=(i - LAG) < 2)
                              for i in range(max(0, len(tiles) - LAG), len(tiles)):
                                  stageB(*tiles[i], first=i < 2)
                              O("act", "copy", reads=[Bob], writes=[Bso[pp][qt]], out=sb_out[:, pp, T4(qt)], in_=ob[:])
                  SC.barrier()

                  if stop == 5:
                      for c in range(2):
                          O("pool", "dma_start", reads=Bdo[c], chan=out_ch, out=dbg_d[:, c, :], in_=dil_out[:, c, :])
                          O("pool", "dma_start", reads=Bso[c], chan=out_ch, out=dbg_d[:, 2 + c, :], in_=sb_out[:, c, :])
                  chk(5)
                  with Safe() as es2:
                      def sb2(name, shape, dt):
                          return es2.enter_context(nc.sbuf_tensor(f"{name}_{l}", list(shape), dt))
                      mixed = sb2("mixed", [128, 8, S], BF16)
                      Bmx = [[Buf() for _ in range(4)] for _ in range(8)]
                      gts = Rot([(sb2(f"gt{i}", [128, 512], F32), Buf()) for i in range(4)])
                      mts = Rot([(sb2(f"mt{i}", [128, 512], F32), Buf()) for i in range(4)])
                      wpab = sb2("wpab", [128, 2, 2, D], BF16)
                      wpa, wpb = wpab[:, 0, :, :], wpab[:, 1, :, :]
                      Bwpa = Bwpb = Buf()
                      pch = Chan(es.enter_context(nc.semaphore(f"pch{l}")))
                      O("pool", "dma_start", writes=[Bwpa], chan=pch, out=wpa, in_=wpa_d[l].rearrange("(k p) n -> p k n", p=128))
                      O("pool", "dma_start", writes=[Bwpb], chan=pch, out=wpb, in_=wpb_d[l].rearrange("(k p) n -> p k n", p=128))
                      for dmc in range(8):
                          (wga, wgb), Bwg = load_unit([wcols(win_d, l, 3072 + dmc * 128, 128), wcols(win_d, l, 4096 + dmc * 128, 128)])
                          for tt in range(4):
                              res = []
                              for (wg, wp, Bwp, src, Bsrc, bcol) in ((wga, wpa, Bwpa, dil_out, Bdo, dmc), (wgb, wpb, Bwpb, sb_out, Bso, 8 + dmc)):
                                  gb_, Bgb = PA.next()
                                  proj_fm(wg, tt, gb_, Bgb, Bwg)
                                  gt, Bgt = gts.next()
                                  O("act", "activation", reads=[Bgb, Bconst], writes=[Bgt], out=gt[:], in_=gb_[:], func=AF.Sigmoid, bias=bgate[:, l, bcol:bcol + 1])
                                  yb, Byb = PA.next()
                                  for c in range(2):
                                      O("pe", "matmul", reads=[Bwp, Bsrc[c][tt]], writes=[Byb], out=yb[:], lhsT=wp[:, c, dmc * 128:(dmc + 1) * 128], rhs=src[:, c, T4(tt)], start=(c == 0), stop=(c == 1))
                                  mt, Bmt = mts.next()
                                  O("dve", "tensor_tensor", reads=[Byb, Bgt], writes=[Bmt], out=mt[:], in0=yb[:], in1=gt[:], op=ALU.mult)
                                  res.append((mt, Bmt))
                              (m1, Bm1), (m2, Bm2) = res
                              O("dve", "tensor_tensor", reads=[Bm1, Bm2], writes=[Bmx[dmc][tt]], out=mixed[:, dmc, T4(tt)], in0=m1[:], in1=m2[:], op=ALU.add)
                      for dp in range(4):
                          (wo,), Bwo = load_unit([wcols(wout_d, l, dp * 256, 256)])
                          for h2 in range(2):
                              dmo = dp * 2 + h2
                              for tt in range(4):
                                  bank, Bb = PA.next()
                                  for c in range(8):
                                      O("pe", "matmul", reads=[Bwo, Bmx[c][tt]], writes=[Bb], out=bank[:], lhsT=wo[:, c, h2 * 128:(h2 + 1) * 128], rhs=mixed[:, c, T4(tt)], start=(c == 0), stop=(c == 7))
                                  O("dve", "tensor_tensor", reads=[Bb, Bx[dmo][tt]], writes=[Bx[dmo][tt]], out=xT[:, dmo, T4(tt)], in0=bank[:], in1=xT[:, dmo, T4(tt)], op=ALU.add)
                  SC.barrier()

              chk(6)
              with Safe() as es3:
                  def sb3(name, shape, dt):
                      return es3.enter_context(nc.sbuf_tensor(f"{name}_{l}", list(shape), dt))
                  wk = {
                      "sq": Rot([(sb3(f"fsq{i}", [128, 512], BF16), Buf()) for i in range(2)]),
                      "rt": Rot([(sb3(f"frt{i}", [128, 512], F32), Buf()) for i in range(2)]),
                  }
                  gbuf = sb3("gbuf", [128, 6, S], BF16)
                  Bg = [[Buf() for _ in range(4)] for _ in range(6)]
                  asb = Rot([(sb3(f"asb{i}", [128, S + 2], F32), [Buf() for _ in range(5)]) for i in range(2)])
                  cts = Rot([(sb3(f"ct{i}", [128, S], F32), Buf()) for i in range(2)])
                  sts = Rot([(sb3(f"st{i}", [128, S], BF16), Buf()) for i in range(2)])
                  for (a_t, a_B) in asb.items:
                      O("dve", "memset", writes=[a_B[4]], ap=a_t[:, 0:2], constant=0.0)
                  norm_to_h(DEPTH + l, wk)
                  for (f0, nf) in FF_PARTS:
                      pendB = []

                      def ffB(wb, Bw, stt, Bst, fi):
                          for tt in range(4):
                              bank, Bb = PA.next()
                              proj_fm(wb, tt, bank, Bb, Bw)
                              O("dve", "tensor_tensor", reads=[Bb, Bst], writes=[Bg[fi][tt]], out=gbuf[:, fi, T4(tt)], in0=bank[:], in1=stt[:, T4(tt)], op=ALU.mult)

                      for fi in range(nf):
                          f = f0 + fi
                          (wa, wb), Bw = load_unit([wcols(wup_d, l, f * 128, 128), wcols(wup_d, l, D_FF + f * 128, 128)])
                          a_t, a_B = asb.next()
                          for tt in range(4):
                              bank, Bb = PA.next()
                              proj_fm(wa, tt, bank, Bb, Bw)
                              O("act", "copy", reads=[Bb], writes=[a_B[tt]], out=a_t[:, 2 + tt * 512:2 + (tt + 1) * 512], in_=bank[:])
                          ct, Bct = cts.next()
                          O("dve", "tensor_scalar", reads=a_B + [Bconst], writes=[Bct], out=ct[:], in0=a_t[:, 2:S + 2], scalar1=convw[:, l, 2, f:f + 1], scalar2=convb[:, l, f:f + 1], op0=ALU.mult, op1=ALU.add)
                          O("dve", "scalar_tensor_tensor", reads=a_B + [Bconst, Bct], writes=[Bct], out=ct[:], in0=a_t[:, 1:S + 1], scalar=convw[:, l, 1, f:f + 1], in1=ct[:], op0=ALU.mult, op1=ALU.add)
                          O("dve", "scalar_tensor_tensor", reads=a_B + [Bconst, Bct], writes=[Bct], out=ct[:], in0=a_t[:, 0:S], scalar=convw[:, l, 0, f:f + 1], in1=ct[:], op0=ALU.mult, op1=ALU.add)
                          stt, Bst = sts.next()
                          O("act", "activation", reads=[Bct], writes=[Bst], out=stt[:], in_=ct[:], func=AF.Silu)
                          if pendB:
                              ffB(*pendB.pop())
                          pendB.append((wb, Bw, stt, Bst, fi))
                      ffB(*pendB.pop())
                      for dp in range(4):
                          (wd,), Bwd = load_unit([(wdn_d[l][f0 * 128:(f0 + nf) * 128, dp * 256:(dp + 1) * 256].rearrange("(k p) n -> p k n", p=128), nf, 256, Bgath[("w_down", l)])])
                          for h2 in range(2):
                              dmo = dp * 2 + h2
                              for tt in range(4):
                                  bank, Bb = PA.next()
                                  for fi in range(nf):
                                      O("pe", "matmul", reads=[Bwd, Bg[fi][tt]], writes=[Bb], out=bank[:], lhsT=wd[:, fi, h2 * 128:(h2 + 1) * 128], rhs=gbuf[:, fi, T4(tt)], start=(fi == 0), stop=(fi == nf - 1))
                                  O("dve", "tensor_tensor", reads=[Bb, Bx[dmo][tt]], writes=[Bx[dmo][tt]], out=xT[:, dmo, T4(tt)], in0=bank[:], in1=xT[:, dmo, T4(tt)], op=ALU.add)
              SC.barrier()

        try:
            layers()
        except StopBuild:
            SC.barrier()

        with contextlib.ExitStack() as es4:
            def sb4(name, shape, dt):
                return es4.enter_context(nc.sbuf_tensor(name, list(shape), dt))
            wk = {
                "sq": Rot([(sb4(f"lsq{i}", [128, 512], BF16), Buf()) for i in range(2)]),
                "rt": Rot([(sb4(f"lrt{i}", [128, 512], F32), Buf()) for i in range(2)]),
            }
            outs = Rot([(sb4(f"ot{i}", [128, 8, 512], F32), Buf()) for i in range(2)])
            outT_v = outT_d.rearrange("(c p) t -> p c t", p=128)
            for tt, rt, Brt in rmsnorm(wk):
                ot, Bot = outs.next()
                for c in range(8):
                    O("dve", "scalar_tensor_tensor", reads=[Bx[c][tt], Brt, Bconst], writes=[Bot],
                      out=ot[:, c, :], in0=xT[:, c, T4(tt)], scalar=gains[:, 2 * DEPTH, c:c + 1], in1=rt[:], op0=ALU.mult, op1=ALU.mult)
                O("sp", "dma_start", reads=[Bot], chan=out_ch, out=outT_v[:, :, T4(tt)], in_=ot[:])

        SC.finalize(lambda name: es.enter_context(nc.semaphore(name)))
        with nc.Block() as block:
            @block.tensor
            def _(e):
                SC.run("pe", e)

            @block.scalar
            def _(e):
                SC.run("act", e)

            @block.vector
            def _(e):
                SC.run("dve", e)

            @block.gpsimd
            def _(e):
                SC.run("pool", e)

            @block.sync
            def _(e):
                SC.run("sp", e)
                e.wait_ge(out_ch.sem, out_ch.count)
    return nc


_NC_CACHE = {}


def kernel(x, positions, norm_mix, w_in, b_gate, w_proj_a, w_proj_b, w_out,
           norm_ffn, w_up, conv_w, conv_b, w_down, norm_final, _n_layers=DEPTH, _n_cores=NC8, _stop=99):
    f = lambda a: np.ascontiguousarray(np.asarray(a, dtype=np.float32))
    x = np.asarray(x, dtype=np.float32)
    positions = np.asarray(positions).astype(np.int32)
    def pc(a, inner):
        a = f(a)
        lead = a.shape[:-1]
        a = a.reshape(lead + (inner, 128))
        a = np.moveaxis(a, -1, 0)
        return np.ascontiguousarray(a.reshape(128, -1))
    gains_h = np.concatenate([pc(norm_mix, 8), pc(norm_ffn, 8), pc(f(norm_final).reshape(1, D), 8)], axis=1)
    shared = {
        "gains_h": np.ascontiguousarray(gains_h), "bgate_h": pc(b_gate, 16),
        "convw_h": pc(conv_w, NFC), "convb_h": pc(conv_b, NFC),
    }
    wfull = {"w_in": f(w_in), "w_proj_a": f(w_proj_a), "w_proj_b": f(w_proj_b), "w_out": f(w_out), "w_up": f(w_up), "w_down": f(w_down)}
    for nm, w in wfull.items():
        for l in range(_n_layers):
            shared[f"{nm}_{l}"] = np.ascontiguousarray(w[l])
    shared.update(_consts())
    if (_n_layers, _stop) not in _NC_CACHE:
        _NC_CACHE[(_n_layers, _stop)] = build(_n_layers, _stop)
    nc = _NC_CACHE[(_n_layers, _stop)]
    in_maps = []
    for c in range(_n_cores):
        m = dict(shared)
        m["xT"] = np.ascontiguousarray(x[c].T)
        m["pos"] = np.ascontiguousarray(positions[c].reshape(1, S))

        in_maps.append(m)
    res = run_bass_kernel_spmd(nc, in_maps, core_ids=list(range(_n_cores)))
    global _DBG
    _DBG = [r.get("dbg") for r in res.results]
    out = np.stack([np.asarray(r["outT"]).T for r in res.results], axis=0)
    return np.ascontiguousarray(out.astype(np.float32))
```

```python
import contextlib
import math

import numpy as np
import ml_dtypes

import concourse.bass as bass
import concourse.mybir as mybir
from concourse.bass_utils import run_bass_kernel_spmd

F32 = mybir.dt.float32
BF16 = mybir.dt.bfloat16
I32 = mybir.dt.int32
AF = mybir.ActivationFunctionType
ALU = mybir.AluOpType

D = 1024
S = 2048
DEPTH = 4
NC8 = 8
IN_COLS = 5120
D_FF = 2816
NFC = 22
EPS = 1e-5
NEG = -30000.0
DIL = ((128, 1), (512, 4), (2048, 16))
FF_PARTS = ((0, 6), (6, 6), (12, 5), (17, 5))
SLOT = 2048
NSLOT = 3


class Buf:
    __slots__ = ("name", "w", "r")

    def __init__(self, name=""):
        self.name = name
        self.w = None
        self.r = []


class Chan:
    def __init__(self, sem):
        self.sem = sem
        self.count = 0


class Op:
    __slots__ = ("eng", "fn", "deps", "ticket", "inc", "chan", "chan_val", "waits")


class Sched:
    ENGS = ("pe", "act", "dve", "pool", "sp")

    def __init__(self):
        self.ops = {e: [] for e in self.ENGS}
        self.all = []
        self.extra = []

    def barrier(self):
        self.extra = [self.ops[e][-1] for e in self.ENGS if self.ops[e]]

    def op(self, eng, meth, reads=(), writes=(), chan=None, nobar=False, **kw):
        o = Op()
        o.eng, o.fn, o.chan = eng, (meth, kw), chan
        o.chan_val = None
        o.inc = False
        o.ticket = None
        deps = [] if nobar else list(self.extra)
        for b in reads:
            if b.w is not None:
                deps.append(b.w)
        for b in writes:
            if b.w is not None:
                deps.append(b.w)
            last = {}
            for r in b.r:
                last[(r.eng, id(r.chan) if r.chan is not None else 0)] = r
            deps.extend(last.values())
        o.deps = deps
        for b in reads:
            b.r.append(o)
        for b in writes:
            b.w = o
            b.r = []
        if chan is not None:
            chan.count += 16
            o.chan_val = chan.count
        self.ops[eng].append(o)
        self.all.append(o)
        return o

    EPOCH = 1500

    def finalize(self, alloc):
        for o in self.all:
            for d in o.deps:
                if d is o or d.chan is not None:
                    continue
                if d.eng == "pe" and o.eng == "pe":
                    continue
                d.inc = True
        sems = {}
        for e in self.ENGS:
            t = 0
            for o in self.ops[e]:
                if o.chan is None and o.inc:
                    t += 1
                    o.ticket = t
            sems[e] = [alloc(f"s_{e}_{i}") for i in range((t + self.EPOCH - 1) // self.EPOCH)]
        E = self.EPOCH
        for e in self.ENGS:
            seen = {}
            for o in self.ops[e]:
                need = {}
                for d in o.deps:
                    if d is o:
                        continue
                    if d.chan is not None:
                        key, sem, val, gval = ("c", id(d.chan)), d.chan.sem, d.chan_val, d.chan_val
                    else:
                        if d.eng == "pe" and o.eng == "pe":
                            continue
                        key, gval = ("e", d.eng), d.ticket
                        sem, val = sems[d.eng][(gval - 1) // E], (gval - 1) % E + 1
                    if seen.get(key, 0) >= gval:
                        continue
                    if key not in need or need[key][2] < gval:
                        need[key] = (sem, val, gval)
                o.waits = [(sem, val) for (sem, val, _) in need.values()]
                for key, (sem, val, gval) in need.items():
                    seen[key] = gval
                o.deps = None
        self.sems = sems

    def run(self, e, eng):
        sems = self.sems
        E = self.EPOCH
        for o in self.ops[e]:
            for sem, val in o.waits:
                eng.wait_ge(sem, val)
            meth, kw = o.fn
            ins = getattr(eng, meth)(**kw)
            if o.chan is not None:
                ins.then_inc(o.chan.sem, 16)
            elif o.inc:
                ins.then_inc(sems[e][(o.ticket - 1) // E], 1)


class Rot:
    def __init__(self, items):
        self.items = items
        self.i = 0

    def next(self):
        it = self.items[self.i % len(self.items)]
        self.i += 1
        return it


def _consts():
    c = {}
    eye = np.eye(128, dtype=np.float32)
    perm = np.zeros((128, 128), np.float32)
    for hb in (0, 64):
        for i in range(8):
            perm[hb + 8 + i, hb + i] = 1.0
            perm[hb + i, hb + 8 + i] = 1.0
    k = np.arange(128)[:, None]
    m = np.arange(128)[None, :]
    tri = (k >= m).astype(np.float32)
    mats = np.stack([eye, -eye, perm, tri, np.ones((128, 128), np.float32)], axis=1)
    c["cmats"] = np.ascontiguousarray(mats.reshape(128, 5 * 128))
    p = np.arange(128)[:, None]
    msk = []
    for (W, dil), width in zip(DIL, (256, 640, 640)):
        x = np.arange(width)[None, :]
        dlt = x - p
        valid = (dlt >= 0) & (dlt <= W) & (dlt % dil == 0)
        msk.append(np.where(valid, 0.0, NEG).astype(np.float32))
    x = np.arange(896)[None, :]
    msk.append(np.where(x - 384 > p, 0.0, NEG).astype(np.float32))
    c["cmask"] = np.ascontiguousarray(np.concatenate(msk, axis=1))
    invf = (500000.0 ** (-np.arange(0, 16, 2, dtype=np.float32) / 16.0)).astype(np.float32)
    col = np.zeros((128, 2), np.float32)
    for hb in (0, 64):
        col[hb:hb + 8, 0] = invf
        col[hb + 8:hb + 16, 0] = invf
        col[hb:hb + 8, 1] = -1.0
        col[hb + 8:hb + 16, 1] = 1.0
    c["ccol"] = col
    return c


MOFF = (0, 256, 896, 1536)
MW = 2432


class StopBuild(Exception):
    pass


class Safe(contextlib.ExitStack):
    def __exit__(self, et, ev, tb):
        super().__exit__(None, None, None)
        return False


def build(n_layers=DEPTH, stop=99):
    nc = bass.Bass("TRN2", target_bir_lowering=False)

    def din(name, shape, dt=F32):
        return nc.dram_tensor(name, list(shape), dt, kind="ExternalInput").ap()

    xT_d = din("xT", [D, S])
    pos_d = din("pos", [1, S], I32)
    gains_d = din("gains_h", [128, (2 * DEPTH + 1) * 8])
    bgate_d = din("bgate_h", [128, DEPTH * 16])
    convw_d = din("convw_h", [128, DEPTH * 3 * NFC])
    convb_d = din("convb_h", [128, DEPTH * NFC])
    WSPEC = (("w_in", D, IN_COLS), ("w_proj_a", 256, D), ("w_proj_b", 256, D), ("w_out", D, D), ("w_up", D, 2 * D_FF), ("w_down", D_FF, D))
    gath_d = {(nm, l): din(f"{nm}_{l}", [rows, cols]) for nm, rows, cols in WSPEC for l in range(n_layers)}

    class _G:
        def __init__(self, nm):
            self.nm = nm

        def __getitem__(self, l):
            return gath_d[(self.nm, l)]
    win_d, wpa_d, wpb_d, wout_d, wup_d, wdn_d = (_G(nm) for nm, _, _ in WSPEC)
    Bgath = {k: Buf() for k in gath_d}
    cmats_d = din("cmats", [128, 640])
    cmask_d = din("cmask", [128, MW])
    ccol_d = din("ccol", [128, 2])
    outT_d = nc.dram_tensor("outT", [D, S], F32, kind="ExternalOutput").ap()
    dbg_d = nc.dram_tensor("dbg", [128, 4, S], F32, kind="ExternalOutput").ap() if stop < 99 else None

    SC = Sched()
    O = SC.op
    es = contextlib.ExitStack()
    with es:
        def sb(name, shape, dt):
            return es.enter_context(nc.sbuf_tensor(name, list(shape), dt))

        xT = sb("xT_sb", [128, 8, S], F32)
        hT = sb("hT_sb", [128, 8, S], BF16)
        ropeC = sb("ropeC", [128, S], BF16)
        ropeS = sb("ropeS", [128, S], BF16)
        cmats = sb("cmats_sb", [128, 5, 128], BF16)
        cmask = sb("cmask_sb", [128, MW], BF16)
        ccol = sb("ccol_sb", [128, 2], F32)
        gains = sb("gains", [128, 2 * DEPTH + 1, 8], F32)
        bgate = sb("bgate", [128, DEPTH, 16], F32)
        convw = sb("convw", [128, DEPTH, 3, NFC], F32)
        convb = sb("convb", [128, DEPTH, NFC], F32)
        ring = sb("ring", [128, NSLOT, SLOT], BF16)
        ident, negident, perm, tri, ones = (cmats[:, i, :] for i in range(5))

        psum = [es.enter_context(nc.psum_tensor(f"ps{i}", [128, 512], F32)) for i in range(8)]
        PA = Rot([(psum[i], Buf(f"ps{i}")) for i in range(5)])
        PB = Rot([(psum[i], Buf(f"ps{i}")) for i in range(5, 8)])

        slot_ch = [Chan(es.enter_context(nc.semaphore(f"slot{i}"))) for i in range(NSLOT)]
        slot_buf = [Buf(f"slot{i}") for i in range(NSLOT)]
        x_ch = [Chan(es.enter_context(nc.semaphore(f"xch{i}"))) for i in range(8)]
        init_ch = Chan(es.enter_context(nc.semaphore("initch")))
        out_ch = Chan(es.enter_context(nc.semaphore("outch")))
        bnc_ch = [Chan(es.enter_context(nc.semaphore(f"bnc{i}"))) for i in range(DEPTH)]

        Bx = [[Buf() for _ in range(4)] for _ in range(8)]
        Bh = [[Buf() for _ in range(4)] for _ in range(8)]
        Bconst = Buf("const")
        Brope = Buf("rope")

        def T4(t):
            return slice(t * 512, (t + 1) * 512)

        slot_i = [0]

        def load_unit(pieces):
            si = slot_i[0] % NSLOT
            slot_i[0] += 1
            b = slot_buf[si]
            lastr = {}
            for r in b.r:
                lastr[r.eng] = r
            old = ([b.w] if b.w is not None else []) + list(lastr.values())
            off = 0
            views = []
            last = None
            for (src, k, n, bsrc) in pieces:
                dst = ring[:, si, off:off + k * n].rearrange("p (k n) -> p k n", k=k)
                tmp = Buf()
                tmp.r = list(old)
                last = O("pool", "dma_start", reads=[bsrc], writes=[tmp], chan=slot_ch[si], nobar=True, out=dst, in_=src)
                views.append(dst)
                off += k * n
            assert off <= SLOT
            b.w = last
            b.r = []
            return views, b

        def wcols(wd, l, c0, n):
            return (wd[l].rearrange("(k p) n -> p k n", p=128)[:, :, c0:c0 + n], 8, n, Bgath[(wd.nm, l)])

        def vec8(src):
            return src.rearrange("(c p) -> p c", p=128)

        O("pool", "dma_start", writes=[Bconst], chan=init_ch, out=cmats[:].rearrange("p a b -> p (a b)"), in_=cmats_d)
        O("pool", "dma_start", writes=[Bconst], chan=init_ch, out=cmask[:], in_=cmask_d)
        O("sp", "dma_start", writes=[Bconst], chan=init_ch, out=ccol[:], in_=ccol_d)
        O("sp", "dma_start", writes=[Bconst], chan=init_ch, out=gains[:].rearrange("p a b -> p (a b)"), in_=gains_d)
        O("sp", "dma_start", writes=[Bconst], chan=init_ch, out=bgate[:].rearrange("p a b -> p (a b)"), in_=bgate_d)
        O("sp", "dma_start", writes=[Bconst], chan=init_ch, out=convw[:].rearrange("p a b c -> p (a b c)"), in_=convw_d)
        O("sp", "dma_start", writes=[Bconst], chan=init_ch, out=convb[:].rearrange("p a b -> p (a b)"), in_=convb_d)
        for c in range(8):
            O("sp", "dma_start", writes=Bx[c], chan=x_ch[c], out=xT[:, c, :], in_=xT_d[c * 128:(c + 1) * 128, :])

        with contextlib.ExitStack() as es0:
            posi = es0.enter_context(nc.sbuf_tensor("posi", [128, S], I32))
            ang = es0.enter_context(nc.sbuf_tensor("ang", [128, S], F32))
            ta = es0.enter_context(nc.sbuf_tensor("ta", [128, S], F32))
            tb = es0.enter_context(nc.sbuf_tensor("tb", [128, S], F32))
            ki = es0.enter_context(nc.sbuf_tensor("ki", [128, S], I32))
            Bp, Bang, Bta, Btb, Bki = Buf(), Buf(), Buf(), Buf(), Buf()
            O("sp", "dma_start", writes=[Bp], chan=init_ch, out=posi[:], in_=pos_d[0, :].partition_broadcast(128))
            O("dve", "tensor_copy", reads=[Bp], writes=[Bang], out=ang[:], in_=posi[:])
            O("dve", "tensor_scalar", reads=[Bang, Bconst], writes=[Bang], out=ang[:], in0=ang[:], scalar1=ccol[:, 0:1], scalar2=None, op0=ALU.mult)

            def sin_table(shift, dst, use_sign):
                O("dve", "tensor_scalar", reads=[Bang], writes=[Bta], out=ta[:], in0=ang[:], scalar1=shift, scalar2=1.0 / (2 * math.pi), op0=ALU.add, op1=ALU.mult)
                O("dve", "tensor_copy", reads=[Bta], writes=[Bki], out=ki[:], in_=ta[:])
                O("dve", "tensor_copy", reads=[Bki], writes=[Btb], out=tb[:], in_=ki[:])
                O("dve", "tensor_scalar", reads=[Bang], writes=[Bta], out=ta[:], in0=ang[:], scalar1=shift, scalar2=None, op0=ALU.add)
                O("dve", "scalar_tensor_tensor", reads=[Btb, Bta], writes=[Bta], out=ta[:], in0=tb[:], scalar=-2 * math.pi, in1=ta[:], op0=ALU.mult, op1=ALU.add)
                O("dve", "tensor_scalar", reads=[Bta], writes=[Btb], out=tb[:], in0=ta[:], scalar1=math.pi, scalar2=-2 * math.pi, op0=ALU.is_gt, op1=ALU.mult)
                O("dve", "tensor_tensor", reads=[Bta, Btb], writes=[Bta], out=ta[:], in0=ta[:], in1=tb[:], op=ALU.add)
                O("dve", "tensor_scalar", reads=[Bta], writes=[Bta], out=ta[:], in0=ta[:], scalar1=-3.1415925, scalar2=3.1415925, op0=ALU.max, op1=ALU.min)
                if use_sign:
                    O("act", "activation", reads=[Bta], writes=[Btb], out=tb[:], in_=ta[:], func=AF.Sin)
                    O("dve", "tensor_scalar", reads=[Btb, Bconst], writes=[Brope], out=dst[:], in0=tb[:], scalar1=ccol[:, 1:2], scalar2=None, op0=ALU.mult)
                else:
                    O("act", "activation", reads=[Bta], writes=[Brope], out=dst[:], in_=ta[:], func=AF.Sin)

            sin_table(0.0, ropeS, True)
            sin_table(math.pi / 2, ropeC, False)
        SC.barrier()

        def rmsnorm(wk):
            res = []
            for tt in range(4):
                bank, Bb = PA.next()
                for c in range(8):
                    sq, Bsq = wk["sq"].next()
                    O("act", "activation", reads=[Bx[c][tt]], writes=[Bsq], out=sq[:], in_=xT[:, c, T4(tt)], func=AF.Square)
                    O("pe", "matmul", reads=[Bsq, Bconst], writes=[Bb], out=bank[:], lhsT=ones, rhs=sq[:], start=(c == 0), stop=(c == 7))
                rt, Brt = wk["rt"].next()
                O("act", "activation", reads=[Bb], writes=[Brt], out=rt[:], in_=bank[:], func=AF.Sqrt, scale=1.0 / D, bias=EPS)
                O("dve", "reciprocal", reads=[Brt], writes=[Brt], out=rt[:], in_=rt[:])
                yield tt, rt, Brt

        def norm_to_h(gidx, wk):
            for tt, rt, Brt in rmsnorm(wk):
                for c in range(8):
                    O("dve", "scalar_tensor_tensor", reads=[Bx[c][tt], Brt, Bconst], writes=[Bh[c][tt]],
                      out=hT[:, c, T4(tt)], in0=xT[:, c, T4(tt)], scalar=gains[:, gidx, c:c + 1], in1=rt[:], op0=ALU.mult, op1=ALU.mult)

        def proj_fm(wview, tt, bank, Bb, Bw):
            for kc in range(8):
                O("pe", "matmul", reads=[Bw, Bh[kc][tt]], writes=[Bb], out=bank[:], lhsT=wview[:, kc, :], rhs=hT[:, kc, T4(tt)], start=(kc == 0), stop=(kc == 7))

        def proj_tm(wview, blk, bank, Bb, Bw, n):
            for kc in range(8):
                O("pe", "matmul", reads=[Bw, Bh[kc][blk // 4]], writes=[Bb], out=bank[:, 0:n], lhsT=hT[:, kc, blk * 128:(blk + 1) * 128], rhs=wview[:, kc, :], start=(kc == 0), stop=(kc == 7))

        def chk(stage):
            if stop <= stage:
                raise StopBuild()

        def layers():
          for l in range(n_layers):
            chk(0)
            for si_ in range(NSLOT):
                slot_ch[si_] = Chan(es.enter_context(nc.semaphore(f"slot{si_}_{l}")))
            if True:
              with Safe() as esm:
                  dil_out = esm.enter_context(nc.sbuf_tensor(f"dil_out_{l}", [128, 2, S], BF16))
                  sb_out = esm.enter_context(nc.sbuf_tensor(f"sb_out_{l}", [128, 2, S], BF16))
                  Bdo = [[Buf() for _ in range(4)] for _ in range(2)]
                  Bso = [[Buf() for _ in range(4)] for _ in range(2)]

                  with Safe() as es1:
                      def sb1(name, shape, dt):
                          return es1.enter_context(nc.sbuf_tensor(f"{name}_{l}", list(shape), dt))
                      qk = sb1("qk", [128, 6, S], BF16)
                      vb = sb1("vb", [128, 16, 6, 64], BF16)
                      Bqk = [[Buf() for _ in range(4)] for _ in range(6)]
                      Bv = [Buf() for _ in range(16)]
                      f32p = Rot([(sb1(f"f32p{i}", [128, 512], F32), Buf()) for i in range(3)])
                      wk = {"sq": Rot([(sb1(f"sq{i}", [128, 512], BF16), Buf()) for i in range(2)]), "rt": f32p}
                      qraw = Rot([(sb1(f"qraw{i}", [128, 512], BF16), Buf()) for i in range(2)])
                      pts = Rot([(sb1(f"pt{i}", [128, 512], BF16), Buf()) for i in range(3)])
                      ets = Rot([(sb1(f"et{i}", [128, 512], F32), Buf()) for i in range(2)])
                      f32p4 = Rot(f32p.items + ets.items[:1])
                      wts = Rot([(sb1(f"wt{i}", [128, 512], BF16), Buf()) for i in range(4)])
                      wacc32 = [(sb1(f"wacc32_{j}", [128, 512], F32), Buf()) for j in range(2)]
                      waccb = [Rot([(sb1(f"waccb{j}_{i}", [128, 512], BF16), Buf()) for i in range(3)]) for j in range(2)]

                      norm_to_h(l, wk)
                      chk(1)

                      for pp in range(2):
                          rope_pend = []

                          add_pend = []

                          def ropeB(qr, Bqr, ci, tt):
                              bank2, Bb2 = PA.next()
                              O("pe", "matmul", reads=[Bqr, Bconst], writes=[Bb2], out=bank2[:], lhsT=perm, rhs=qr[:], start=True, stop=True)
                              t1, Bt1 = f32p4.next()
                              t2, Bt2 = f32p4.next()
                              O("dve", "tensor_tensor", reads=[Bqr, Brope], writes=[Bt1], out=t1[:], in0=qr[:], in1=ropeC[:, T4(tt)], op=ALU.mult)
                              O("dve", "tensor_tensor", reads=[Bb2, Brope], writes=[Bt2], out=t2[:], in0=bank2[:], in1=ropeS[:, T4(tt)], op=ALU.mult)
                              if add_pend:
                                  ropeC_(*add_pend.pop())
                              add_pend.append((t1, Bt1, t2, Bt2, ci, tt))

                          def ropeC_(t1, Bt1, t2, Bt2, ci, tt):
                              O("dve", "tensor_tensor", reads=[Bt1, Bt2], writes=[Bqk[ci][tt]], out=qk[:, ci, T4(tt)], in0=t1[:], in1=t2[:], op=ALU.add)

                          for g in range(3):
                              (wq, wkk), Bw = load_unit([wcols(win_d, l, g * 256 + 128 * pp, 128), wcols(win_d, l, 768 + g * 256 + 128 * pp, 128)])
                              for ci, wv_ in ((g, wq), (3 + g, wkk)):
                                  for tt in range(4):
                                      bank, Bb = PA.next()
                                      proj_fm(wv_, tt, bank, Bb, Bw)
                                      qr, Bqr = qraw.next()
                                      O("act", "copy", reads=[Bb], writes=[Bqr], out=qr[:], in_=bank[:])
                                      if rope_pend:
                                          ropeB(*rope_pend.pop())
                                      rope_pend.append((qr, Bqr, ci, tt))
                          ropeB(*rope_pend.pop())
                          ropeC_(*add_pend.pop())
                          vv01, Bw01 = load_unit([wcols(win_d, l, 1536 + g * 256 + 128 * pp, 128) for g in range(2)])
                          vv2, Bw2 = load_unit([wcols(win_d, l, 1536 + 512 + 128 * pp, 128)])
                          vviews = [(vv01[0], Bw01), (vv01[1], Bw01), (vv2[0], Bw2)]
                          for blk in range(16):
                              for g in range(3):
                                  bank, Bb = PA.next()
                                  proj_tm(vviews[g][0], blk, bank, Bb, vviews[g][1], 128)
                                  O("act", "copy", reads=[Bb], writes=[Bv[blk]], out=vb[:, blk, 2 * g:2 * g + 2, :], in_=bank[:, 0:128].rearrange("p (h d) -> p h d", h=2))
                          chk(2)
                          for qt in range(4):
                              ob, Bob = PB.next()
                              db, Bdb = PB.next()
                              first = [True, True]
                              dtiles = []
                              for g, (W, dil) in enumerate(DIL):
                                  kb_lo = max(0, (qt * 512 - W) // 128)
                                  kb_hi = qt * 4 + 3
                                  for kb in range(kb_lo, kb_hi + 1):
                                      qs = max(qt * 512, kb * 128)
                                      qe = min((qt + 1) * 512, kb * 128 + W + 128)
                                      n = qe - qs
                                      if n <= 0:
                                          continue
                                      x0 = qs - kb * 128
                                      if g == 2 and x0 >= 128:
                                          x0 = 128
                                      for j in range(2):
                                          dtiles.append((g, kb, j, qs, qe, n, MOFF[g] + x0, qs - qt * 512))
                              dst_ = {}

                              def dA(i):
                                  g, kb, j, qs, qe, n, mo, c0 = dtiles[i]
                                  r0 = j * 64
                                  bank, Bb = PA.next()
                                  O("pe", "matmul", reads=[Bqk[3 + g][kb // 4], Bqk[g][qt]], writes=[Bb], out=bank[:, 0:n],
                                    lhsT=qk[r0:r0 + 64, 3 + g, kb * 128:(kb + 1) * 128], rhs=qk[r0:r0 + 64, g, qs:qe], start=True, stop=False)
                                  O("pe", "matmul", reads=[Bconst], writes=[Bb], out=bank[:, 0:n], lhsT=ident, rhs=cmask[:, mo:mo + n], start=False, stop=True)
                                  pt, Bpt = pts.next()
                                  O("act", "activation", reads=[Bb], writes=[Bpt], out=pt[:, 0:n], in_=bank[:, 0:n], func=AF.Exp, scale=0.125)
                                  dst_[i] = (pt, Bpt)

                              def dB(i):
                                  g, kb, j, qs, qe, n, mo, c0 = dtiles[i]
                                  r0 = j * 64
                                  pt, Bpt = dst_.pop(i)
                                  st = first[j]
                                  first[j] = False
                                  O("pe", "matmul", reads=[Bpt, Bv[kb]], writes=[Bob], out=ob[r0:r0 + 64, c0:c0 + n], lhsT=vb[:, kb, 2 * g + j, :], rhs=pt[:, 0:n],
                                    start=st, stop=False, skip_group_check=True, tile_position=(0, r0))
                                  O("pe", "matmul", reads=[Bpt, Bconst], writes=[Bdb], out=db[r0:r0 + 64, c0:c0 + n], lhsT=ones[:, 0:64], rhs=pt[:, 0:n],
                                    start=st, stop=False, skip_group_check=True, tile_position=(0, r0))

                              DLAG = 2
                              for i in range(len(dtiles)):
                                  dA(i)
                                  if i >= DLAG:
                                      dB(i - DLAG)
                              for i in range(max(0, len(dtiles) - DLAG), len(dtiles)):
                                  dB(i)
                              rd, Brd = f32p.next()
                              O("dve", "reciprocal", reads=[Bdb], writes=[Brd], out=rd[:], in_=db[:])
                              O("dve", "tensor_tensor", reads=[Bob, Brd], writes=[Bdo[pp][qt]], out=dil_out[:, pp, T4(qt)], in0=ob[:], in1=rd[:], op=ALU.mult)
                          chk(3)

                      chk(4)
                      for pp in range(2):
                          (wsq, wsk), Bwqk = load_unit([wcols(win_d, l, 2304 + 128 * pp, 128), wcols(win_d, l, 2560 + 128 * pp, 128)])
                          (wsv,), Bwv = load_unit([wcols(win_d, l, 2816 + 128 * pp, 128)])
                          for tt in range(4):
                              bank, Bb = PA.next()
                              proj_fm(wsq, tt, bank, Bb, Bwqk)
                              O("act", "copy", reads=[Bb], writes=[Bqk[0][tt]], out=qk[:, 0, T4(tt)], in_=bank[:])
                              bank, Bb = PA.next()
                              proj_fm(wsk, tt, bank, Bb, Bwqk)
                              O("act", "copy", reads=[Bb], writes=[Bqk[1][tt]], out=qk[:, 1, T4(tt)], in_=bank[:])
                              O("dve", "tensor_scalar", reads=[Bqk[1][tt]], writes=[Bqk[2][tt]], out=qk[:, 2, T4(tt)], in0=qk[:, 1, T4(tt)], scalar1=-0.125, scalar2=None, op0=ALU.mult)
                          for blk in range(16):
                              bank, Bb = PA.next()
                              proj_tm(wsv, blk, bank, Bb, Bwv, 128)
                              O("act", "copy", reads=[Bb], writes=[Bv[blk]], out=vb[:, blk, 0:2, :], in_=bank[:, 0:128].rearrange("p (h d) -> p h d", h=2))
                          for qt in range(4):
                              ob, Bob = PB.next()
                              kbs = list(range(qt * 4 + 3, -1, -1))
                              stt_ = {}
                              accv = [None, None]

                              def stageA(kb, qt=qt, stt_=stt_, accv=accv):
                                  m = kb - 4 * qt
                                  zs, es_, ws = [], [], []
                                  for j in range(2):
                                      r0 = j * 64
                                      zb, Bz = PA.next()
                                      O("pe", "matmul", reads=[Bqk[1][kb // 4], Bqk[0][qt]], writes=[Bz], out=zb[:],
                                        lhsT=qk[r0:r0 + 64, 1, kb * 128:(kb + 1) * 128], rhs=qk[r0:r0 + 64, 0, T4(qt)], start=True, stop=(m < 0))
                                      if m >= 0:
                                          mo = MOFF[3] + 384 - 128 * m
                                          O("pe", "matmul", reads=[Bconst], writes=[Bz], out=zb[:], lhsT=ident, rhs=cmask[:, mo:mo + 512], start=False, stop=True)
                                      zs.append((zb, Bz))
                                  for j in range(2):
                                      zb, Bz = zs[j]
                                      et, Bet = ets.next()
                                      O("act", "activation", reads=[Bz], writes=[Bet], out=et[:], in_=zb[:], func=AF.Exp, scale=0.125)
                                      es_.append((et, Bet))
                                  for j in range(2):
                                      et, Bet = es_[j]
                                      wt, Bwt = wts.next()
                                      O("act", "activation", reads=[Bet], writes=[Bwt], out=wt[:], in_=et[:], func=AF.Ln, bias=1.0)
                                      ws.append((wt, Bwt))
                                      stt_[(kb, j)] = (wt, Bwt, accv[j])
                                  if kb > 0:
                                      for j in range(2):
                                          wt, Bwt = ws[j]
                                          a32, Ba32 = wacc32[j]
                                          if accv[j] is None:
                                              O("dve", "tensor_copy", reads=[Bwt], writes=[Ba32], out=a32[:], in_=wt[:])
                                          else:
                                              O("dve", "tensor_tensor", reads=[Bwt, Ba32], writes=[Ba32], out=a32[:], in0=a32[:], in1=wt[:], op=ALU.add)
                                      for j in range(2):
                                          a32, Ba32 = wacc32[j]
                                          ab, Bab = waccb[j].next()
                                          O("dve", "tensor_copy", reads=[Ba32], writes=[Bab], out=ab[:], in_=a32[:])
                                          accv[j] = (ab, Bab)

                              def stageB(kb, first, qt=qt, stt_=stt_, ob=ob, Bob=Bob):
                                  m = kb - 4 * qt
                                  rs, ps_ = [], []
                                  for j in range(2):
                                      r0 = j * 64
                                      wt, Bwt, prev = stt_.pop((kb, j))
                                      rb, Br = PA.next()
                                      O("pe", "matmul", reads=[Bwt, Bconst], writes=[Br], out=rb[:], lhsT=tri, rhs=wt[:], start=True, stop=False)
                                      if prev is not None:
                                          ab, Bab = prev
                                          O("pe", "matmul", reads=[Bab, Bconst], writes=[Br], out=rb[:], lhsT=ones, rhs=ab[:], start=False, stop=False)
                                      O("pe", "matmul", reads=[Bqk[2][kb // 4], Bqk[0][qt]], writes=[Br], out=rb[:],
                                        lhsT=qk[r0:r0 + 64, 2, kb * 128:(kb + 1) * 128], rhs=qk[r0:r0 + 64, 0, T4(qt)], start=False, stop=(m < 0))
                                      if m >= 0:
                                          mo = MOFF[3] + 384 - 128 * m
                                          O("pe", "matmul", reads=[Bconst], writes=[Br], out=rb[:], lhsT=negident, rhs=cmask[:, mo:mo + 512], start=False, stop=True)
                                      rs.append((rb, Br))
                                  for j in range(2):
                                      rb, Br = rs[j]
                                      pt, Bpt = pts.next()
                                      O("act", "activation", reads=[Br], writes=[Bpt], out=pt[:], in_=rb[:], func=AF.Exp, scale=-1.0)
                                      ps_.append((pt, Bpt))
                                  for j in range(2):
                                      r0 = j * 64
                                      pt, Bpt = ps_[j]
                                      O("pe", "matmul", reads=[Bpt, Bv[kb]], writes=[Bob], out=ob[r0:r0 + 64, :], lhsT=vb[:, kb, j, :], rhs=pt[:],
                                        start=first, stop=False, skip_group_check=True, tile_position=(0, r0))

                              LAG = 1
                              for i, kb in enumerate(kbs):
                                  stageA(kb)
                                  if i >= LAG:
                                      stageB(kbs[i - LAG], first=(i - LAG) == 0)
                              for i in range(max(0, len(kbs) - LAG), len(kbs)):
                                  stageB(kbs[i], first=(i == 0))
                              O("act", "copy", reads=[Bob], writes=[Bso[pp][qt]], out=sb_out[:, pp, T4(qt)], in_=ob[:])
                  SC.barrier()

                  if stop == 5:
                      for c in range(2):
                          O("pool", "dma_start", reads=Bdo[c], chan=out_ch, out=dbg_d[:, c, :], in_=dil_out[:, c, :])
                          O("pool", "dma_start", reads=Bso[c], chan=out_ch, out=dbg_d[:, 2 + c, :], in_=sb_out[:, c, :])
                  chk(5)
                  with Safe() as es2:
                      def sb2(name, shape, dt):
                          return es2.enter_context(nc.sbuf_tensor(f"{name}_{l}", list(shape), dt))
                      mixed = sb2("mixed", [128, 8, S], BF16)
                      Bmx = [[Buf() for _ in range(4)] for _ in range(8)]
                      gts = Rot([(sb2(f"gt{i}", [128, 512], F32), Buf()) for i in range(4)])
                      mts = Rot([(sb2(f"mt{i}", [128, 512], F32), Buf()) for i in range(4)])
                      wpab = sb2("wpab", [128, 2, 2, D], BF16)
                      wpa, wpb = wpab[:, 0, :, :], wpab[:, 1, :, :]
                      Bwpa = Bwpb = Buf()
                      pch = Chan(es.enter_context(nc.semaphore(f"pch{l}")))
                      O("pool", "dma_start", writes=[Bwpa], chan=pch, out=wpa, in_=wpa_d[l].rearrange("(k p) n -> p k n", p=128))
                      O("pool", "dma_start", writes=[Bwpb], chan=pch, out=wpb, in_=wpb_d[l].rearrange("(k p) n -> p k n", p=128))
                      for dmc in range(8):
                          (wga, wgb), Bwg = load_unit([wcols(win_d, l, 3072 + dmc * 128, 128), wcols(win_d, l, 4096 + dmc * 128, 128)])
                          for tt in range(4):
                              res = []
                              for (wg, wp, Bwp, src, Bsrc, bcol) in ((wga, wpa, Bwpa, dil_out, Bdo, dmc), (wgb, wpb, Bwpb, sb_out, Bso, 8 + dmc)):
                                  gb_, Bgb = PA.next()
                                  proj_fm(wg, tt, gb_, Bgb, Bwg)
                                  gt, Bgt = gts.next()
                                  O("act", "activation", reads=[Bgb, Bconst], writes=[Bgt], out=gt[:], in_=gb_[:], func=AF.Sigmoid, bias=bgate[:, l, bcol:bcol + 1])
                                  yb, Byb = PA.next()
                                  for c in range(2):
                                      O("pe", "matmul", reads=[Bwp, Bsrc[c][tt]], writes=[Byb], out=yb[:], lhsT=wp[:, c, dmc * 128:(dmc + 1) * 128], rhs=src[:, c, T4(tt)], start=(c == 0), stop=(c == 1))
                                  mt, Bmt = mts.next()
                                  O("dve", "tensor_tensor", reads=[Byb, Bgt], writes=[Bmt], out=mt[:], in0=yb[:], in1=gt[:], op=ALU.mult)
                                  res.append((mt, Bmt))
                              (m1, Bm1), (m2, Bm2) = res
                              O("dve", "tensor_tensor", reads=[Bm1, Bm2], writes=[Bmx[dmc][tt]], out=mixed[:, dmc, T4(tt)], in0=m1[:], in1=m2[:], op=ALU.add)
                      for dp in range(4):
                          (wo,), Bwo = load_unit([wcols(wout_d, l, dp * 256, 256)])
                          for h2 in range(2):
                              dmo = dp * 2 + h2
                              for tt in range(4):
                                  bank, Bb = PA.next()
                                  for c in range(8):
                                      O("pe", "matmul", reads=[Bwo, Bmx[c][tt]], writes=[Bb], out=bank[:], lhsT=wo[:, c, h2 * 128:(h2 + 1) * 128], rhs=mixed[:, c, T4(tt)], start=(c == 0), stop=(c == 7))
                                  O("dve", "tensor_tensor", reads=[Bb, Bx[dmo][tt]], writes=[Bx[dmo][tt]], out=xT[:, dmo, T4(tt)], in0=bank[:], in1=xT[:, dmo, T4(tt)], op=ALU.add)
                  SC.barrier()

              chk(6)
              with Safe() as es3:
                  def sb3(name, shape, dt):
                      return es3.enter_context(nc.sbuf_tensor(f"{name}_{l}", list(shape), dt))
                  wk = {
                      "sq": Rot([(sb3(f"fsq{i}", [128, 512], BF16), Buf()) for i in range(2)]),
                      "rt": Rot([(sb3(f"frt{i}", [128, 512], F32), Buf()) for i in range(2)]),
                  }
                  gbuf = sb3("gbuf", [128, 6, S], BF16)
                  Bg = [[Buf() for _ in range(4)] for _ in range(6)]
                  asb = Rot([(sb3(f"asb{i}", [128, S + 2], F32), [Buf() for _ in range(5)]) for i in range(2)])
                  cts = Rot([(sb3(f"ct{i}", [128, S], F32), Buf()) for i in range(2)])
                  sts = Rot([(sb3(f"st{i}", [128, S], BF16), Buf()) for i in range(2)])
                  for (a_t, a_B) in asb.items:
                      O("dve", "memset", writes=[a_B[4]], ap=a_t[:, 0:2], constant=0.0)
                  norm_to_h(DEPTH + l, wk)
                  for (f0, nf) in FF_PARTS:
                      pendB = []

                      def ffB(wb, Bw, stt, Bst, fi):
                          for tt in range(4):
                              bank, Bb = PA.next()
                              proj_fm(wb, tt, bank, Bb, Bw)
                              O("dve", "tensor_tensor", reads=[Bb, Bst], writes=[Bg[fi][tt]], out=gbuf[:, fi, T4(tt)], in0=bank[:], in1=stt[:, T4(tt)], op=ALU.mult)

                      for fi in range(nf):
                          f = f0 + fi
                          (wa, wb), Bw = load_unit([wcols(wup_d, l, f * 128, 128), wcols(wup_d, l, D_FF + f * 128, 128)])
                          a_t, a_B = asb.next()
                          for tt in range(4):
                              bank, Bb = PA.next()
                              proj_fm(wa, tt, bank, Bb, Bw)
                              O("act", "copy", reads=[Bb], writes=[a_B[tt]], out=a_t[:, 2 + tt * 512:2 + (tt + 1) * 512], in_=bank[:])
                          ct, Bct = cts.next()
                          O("dve", "tensor_scalar", reads=a_B + [Bconst], writes=[Bct], out=ct[:], in0=a_t[:, 2:S + 2], scalar1=convw[:, l, 2, f:f + 1], scalar2=convb[:, l, f:f + 1], op0=ALU.mult, op1=ALU.add)
                          O("dve", "scalar_tensor_tensor", reads=a_B + [Bconst, Bct], writes=[Bct], out=ct[:], in0=a_t[:, 1:S + 1], scalar=convw[:, l, 1, f:f + 1], in1=ct[:], op0=ALU.mult, op1=ALU.add)
                          O("dve", "scalar_tensor_tensor", reads=a_B + [Bconst, Bct], writes=[Bct], out=ct[:], in0=a_t[:, 0:S], scalar=convw[:, l, 0, f:f + 1], in1=ct[:], op0=ALU.mult, op1=ALU.add)
                          stt, Bst = sts.next()
                          O("act", "activation", reads=[Bct], writes=[Bst], out=stt[:], in_=ct[:], func=AF.Silu)
                          if pendB:
                              ffB(*pendB.pop())
                          pendB.append((wb, Bw, stt, Bst, fi))
                      ffB(*pendB.pop())
                      for dp in range(4):
                          (wd,), Bwd = load_unit([(wdn_d[l][f0 * 128:(f0 + nf) * 128, dp * 256:(dp + 1) * 256].rearrange("(k p) n -> p k n", p=128), nf, 256, Bgath[("w_down", l)])])
                          for h2 in range(2):
                              dmo = dp * 2 + h2
                              for tt in range(4):
                                  bank, Bb = PA.next()
                                  for fi in range(nf):
                                      O("pe", "matmul", reads=[Bwd, Bg[fi][tt]], writes=[Bb], out=bank[:], lhsT=wd[:, fi, h2 * 128:(h2 + 1) * 128], rhs=gbuf[:, fi, T4(tt)], start=(fi == 0), stop=(fi == nf - 1))
                                  O("dve", "tensor_tensor", reads=[Bb, Bx[dmo][tt]], writes=[Bx[dmo][tt]], out=xT[:, dmo, T4(tt)], in0=bank[:], in1=xT[:, dmo, T4(tt)], op=ALU.add)
              SC.barrier()

        try:
            layers()
        except StopBuild:
            SC.barrier()

        with contextlib.ExitStack() as es4:
            def sb4(name, shape, dt):
                return es4.enter_context(nc.sbuf_tensor(name, list(shape), dt))
            wk = {
                "sq": Rot([(sb4(f"lsq{i}", [128, 512], BF16), Buf()) for i in range(2)]),
                "rt": Rot([(sb4(f"lrt{i}", [128, 512], F32), Buf()) for i in range(2)]),
            }
            outs = Rot([(sb4(f"ot{i}", [128, 8, 512], F32), Buf()) for i in range(2)])
            outT_v = outT_d.rearrange("(c p) t -> p c t", p=128)
            for tt, rt, Brt in rmsnorm(wk):
                ot, Bot = outs.next()
                for c in range(8):
                    O("dve", "scalar_tensor_tensor", reads=[Bx[c][tt], Brt, Bconst], writes=[Bot],
                      out=ot[:, c, :], in0=xT[:, c, T4(tt)], scalar=gains[:, 2 * DEPTH, c:c + 1], in1=rt[:], op0=ALU.mult, op1=ALU.mult)
                O("sp", "dma_start", reads=[Bot], chan=out_ch, out=outT_v[:, :, T4(tt)], in_=ot[:])

        SC.finalize(lambda name: es.enter_context(nc.semaphore(name)))
        with nc.Block() as block:
            @block.tensor
            def _(e):
                SC.run("pe", e)

            @block.scalar
            def _(e):
                SC.run("act", e)

            @block.vector
            def _(e):
                SC.run("dve", e)

            @block.gpsimd
            def _(e):
                SC.run("pool", e)

            @block.sync
            def _(e):
                SC.run("sp", e)
                e.wait_ge(out_ch.sem, out_ch.count)
    return nc


_NC_CACHE = {}


def kernel(x, positions, norm_mix, w_in, b_gate, w_proj_a, w_proj_b, w_out,
           norm_ffn, w_up, conv_w, conv_b, w_down, norm_final, _n_layers=DEPTH, _n_cores=NC8, _stop=99):
    f = lambda a: np.ascontiguousarray(np.asarray(a, dtype=np.float32))
    x = np.asarray(x, dtype=np.float32)
    positions = np.asarray(positions).astype(np.int32)
    def pc(a, inner):
        a = f(a)
        lead = a.shape[:-1]
        a = a.reshape(lead + (inner, 128))
        a = np.moveaxis(a, -1, 0)
        return np.ascontiguousarray(a.reshape(128, -1))
    gains_h = np.concatenate([pc(norm_mix, 8), pc(norm_ffn, 8), pc(f(norm_final).reshape(1, D), 8)], axis=1)
    shared = {
        "gains_h": np.ascontiguousarray(gains_h), "bgate_h": pc(b_gate, 16),
        "convw_h": pc(conv_w, NFC), "convb_h": pc(conv_b, NFC),
    }
    wfull = {"w_in": f(w_in), "w_proj_a": f(w_proj_a), "w_proj_b": f(w_proj_b), "w_out": f(w_out), "w_up": f(w_up), "w_down": f(w_down)}
    for nm, w in wfull.items():
        for l in range(_n_layers):
            shared[f"{nm}_{l}"] = np.ascontiguousarray(w[l])
    shared.update(_consts())
    if (_n_layers, _stop) not in _NC_CACHE:
        _NC_CACHE[(_n_layers, _stop)] = build(_n_layers, _stop)
    nc = _NC_CACHE[(_n_layers, _stop)]
    in_maps = []
    for c in range(_n_cores):
        m = dict(shared)
        m["xT"] = np.ascontiguousarray(x[c].T)
        m["pos"] = np.ascontiguousarray(positions[c].reshape(1, S))

        in_maps.append(m)
    res = run_bass_kernel_spmd(nc, in_maps, core_ids=list(range(_n_cores)))
    global _DBG
    _DBG = [r.get("dbg") for r in res.results]
    out = np.stack([np.asarray(r["outT"]).T for r in res.results], axis=0)
    return np.ascontiguousarray(out.astype(np.float32))
```
